# Optimizing a Trainium2 kernel written in Bass

```python
import math
import jax, jax.numpy as jnp
from jax import lax
import numpy as np

D_MODEL = 2048
BATCH = 4
SEQ = 4096
DEPTH = 2

D_MIX = D_MODEL
D_HYENA = D_MIX // 4
D_SCONV = D_MIX // 4
N_HEADS = 16
N_KV_HEADS = 4
GROUP = N_HEADS // N_KV_HEADS
HEAD_DIM = 64
D_ATTN = N_HEADS * HEAD_DIM
D_KV = N_KV_HEADS * HEAD_DIM
SPLIT_SC = 3 * D_HYENA
SPLIT_Q = SPLIT_SC + 3 * D_SCONV
SPLIT_K = SPLIT_Q + D_ATTN
SPLIT_V = SPLIT_K + D_KV
D_IN_PROJ = SPLIT_V + D_KV
D_FF = 5632
WINDOW = 128
BLOCK = 128
CONV_WIDTH = 3
FILTER_BANDS = 16
FILTER_EMB_DIM = 1 + 2 * FILTER_BANDS
FILTER_HIDDEN = 64
HYENA_FAST_DECAY = 0.3
HYENA_SLOW_DECAY = 1.5
HYENA_DECAY_TARGET = 1e-2
HYENA_MIN_DECAY = math.log(HYENA_DECAY_TARGET) / HYENA_SLOW_DECAY
HYENA_MAX_DECAY = math.log(HYENA_DECAY_TARGET) / HYENA_FAST_DECAY
HYENA_SHIFT = 0.05
EPS = 1e-6
NEG_INF = -1e30

kernel_name = "hybrid_hyena_shortconv_swa_macaron"


def rms_norm(x, g):
    xf = x.astype(jnp.float32)
    y = xf * lax.rsqrt(jnp.mean(xf * xf, axis=-1, keepdims=True) + EPS)
    return (y * g.astype(jnp.float32)).astype(x.dtype)


def swiglu(x, w_gate, w_up, w_down):
    return (jax.nn.silu(x @ w_gate) * (x @ w_up)) @ w_down


def centered_conv3(u, w):
    up = jnp.pad(u, ((0, 0), (1, 1), (0, 0)))
    return up[:, :-2] * w[0] + up[:, 1:-1] * w[1] + up[:, 2:] * w[2]


def hyena_filter_kernel(L, w1, b1, w2, b2, w3, b3, freq, w_out):
    f32 = jnp.float32
    n = jnp.arange(L, dtype=f32)
    t = n / float(max(L - 1, 1))
    bands = jnp.linspace(1e-4, FILTER_BANDS - 1, FILTER_BANDS, dtype=f32)
    ang = (2.0 * math.pi / L) * n[:, None] * bands[None, :]
    z = jnp.concatenate([t[:, None], jnp.cos(ang), jnp.sin(ang)], axis=-1)
    fr = freq.astype(f32)
    h = jnp.sin(fr * (z @ w1.astype(f32) + b1.astype(f32)))
    h = jnp.sin(fr * (h @ w2.astype(f32) + b2.astype(f32)))
    h = jnp.sin(fr * (h @ w3.astype(f32) + b3.astype(f32)))
    h = h @ w_out.astype(f32)
    deltas = jnp.abs(jnp.linspace(HYENA_MIN_DECAY, HYENA_MAX_DECAY, D_HYENA, dtype=f32))
    window = jnp.exp(-t[:, None] * deltas[None, :]) + HYENA_SHIFT
    h_fwd = h[:, :D_HYENA] * window
    h_bwd = h[:, D_HYENA:] * window
    return jnp.concatenate([h_fwd, jnp.zeros((1, D_HYENA), f32), h_bwd[:0:-1]], axis=0)


def two_sided_fftconv(u, kernel):
    L = u.shape[1]
    uf = jnp.fft.rfft(u.astype(jnp.float32), n=2 * L, axis=1)
    kf = jnp.fft.rfft(kernel, n=2 * L, axis=0)
    return jnp.fft.irfft(uf * kf[None], n=2 * L, axis=1)[:, :L]


def hyena_mixer(p, short_w, short_b, bias_d, w1, b1, w2, b2, w3, b3, freq, w_out):
    L = p.shape[1]
    u = centered_conv3(p, short_w) + short_b
    x0, x1, v = jnp.split(u, 3, axis=-1)
    zv = v * x1
    kernel = hyena_filter_kernel(L, w1, b1, w2, b2, w3, b3, freq, w_out)
    y = two_sided_fftconv(zv, kernel) + zv.astype(jnp.float32) * bias_d.astype(jnp.float32)
    return x0 * y.astype(x0.dtype)


def short_conv_mixer(p, conv_w):
    gb, gc, hv = jnp.split(p, 3, axis=-1)
    return gb * centered_conv3(gc * hv, conv_w)


def band_blocks(t, nb):
    tp = jnp.pad(t, ((0, 0), (BLOCK, BLOCK), (0, 0), (0, 0)))
    tp = tp.reshape(t.shape[0], nb + 2, BLOCK, t.shape[2], t.shape[3])
    return jnp.concatenate([tp[:, :-2], tp[:, 1:-1], tp[:, 2:]], axis=2)


def windowed_gqa(q, k, v, q_g, k_g, sink):
    f32 = jnp.float32
    Bsz, L, _ = q.shape
    nb = L // BLOCK
    q = rms_norm(q.reshape(Bsz, L, N_HEADS, HEAD_DIM), q_g)
    k = rms_norm(k.reshape(Bsz, L, N_KV_HEADS, HEAD_DIM), k_g)
    v = v.reshape(Bsz, L, N_KV_HEADS, HEAD_DIM)
    qb = q.reshape(Bsz, nb, BLOCK, N_KV_HEADS, GROUP, HEAD_DIM).astype(f32)
    kb = band_blocks(k, nb).astype(f32)
    vb = band_blocks(v, nb).astype(f32)
    scores = jnp.einsum("bnqhgd,bnshd->bnhgqs", qb, kb) * (HEAD_DIM ** -0.5)
    qpos = jnp.arange(nb)[:, None] * BLOCK + jnp.arange(BLOCK)[None, :]
    kpos = (jnp.arange(nb)[:, None] - 1) * BLOCK + jnp.arange(3 * BLOCK)[None, :]
    dist = jnp.abs(qpos[:, :, None] - kpos[:, None, :])
    valid = (dist <= WINDOW) & (kpos[:, None, :] >= 0) & (kpos[:, None, :] < L)
    slopes = jnp.exp2(-8.0 * jnp.arange(1, N_HEADS + 1, dtype=f32) / N_HEADS)
    slopes = slopes.reshape(N_KV_HEADS, GROUP)[None, None, :, :, None, None]
    logits = scores - slopes * dist.astype(f32)[None, :, None, None]
    logits = jnp.where(valid[None, :, None, None], logits, NEG_INF)
    sink_b = sink.astype(f32).reshape(N_KV_HEADS, GROUP)[None, None, :, :, None]
    m = jnp.maximum(jnp.max(logits, axis=-1), sink_b)
    p = jnp.exp(logits - m[..., None])
    denom = jnp.sum(p, axis=-1) + jnp.exp(sink_b - m)
    probs = p / denom[..., None]
    o = jnp.einsum("bnhgqs,bnshd->bnqhgd", probs, vb)
    return o.reshape(Bsz, L, D_ATTN).astype(q.dtype)


def setup_inputs(seed: int = 0) -> dict:
    key = jax.random.key(seed)
    ks = jax.random.split(key, 28)
    f32 = jnp.float32

    def nrm(k, shape, scale):
        return scale * jax.random.normal(k, shape, f32)

    def gain(k, shape):
        return 1.0 + 0.02 * jax.random.normal(k, shape, f32)

    L_ = DEPTH
    return {
        "x": nrm(ks[0], (BATCH, SEQ, D_MODEL), 1.0),
        "norm_ffn1": gain(ks[1], (L_, D_MODEL)),
        "ffn1_w_gate": nrm(ks[2], (L_, D_MODEL, D_FF), D_MODEL ** -0.5),
        "ffn1_w_up": nrm(ks[3], (L_, D_MODEL, D_FF), D_MODEL ** -0.5),
        "ffn1_w_down": nrm(ks[4], (L_, D_FF, D_MODEL), D_FF ** -0.5),
        "norm_mix": gain(ks[5], (L_, D_MODEL)),
        "w_in": nrm(ks[6], (L_, D_MODEL, D_IN_PROJ), D_MODEL ** -0.5),
        "hyena_short_w": nrm(ks[7], (L_, CONV_WIDTH, 3 * D_HYENA), CONV_WIDTH ** -0.5),
        "hyena_short_b": nrm(ks[8], (L_, 3 * D_HYENA), 0.02),
        "filt_w1": nrm(ks[9], (L_, FILTER_EMB_DIM, FILTER_HIDDEN), FILTER_EMB_DIM ** -0.5),
        "filt_b1": nrm(ks[10], (L_, FILTER_HIDDEN), 0.1),
        "filt_w2": nrm(ks[11], (L_, FILTER_HIDDEN, FILTER_HIDDEN), FILTER_HIDDEN ** -0.5),
        "filt_b2": nrm(ks[12], (L_, FILTER_HIDDEN), 0.1),
        "filt_w3": nrm(ks[13], (L_, FILTER_HIDDEN, FILTER_HIDDEN), FILTER_HIDDEN ** -0.5),
        "filt_b3": nrm(ks[14], (L_, FILTER_HIDDEN), 0.1),
        "filt_freq": gain(ks[15], (L_, FILTER_HIDDEN)),
        "filt_w_out": nrm(ks[16], (L_, FILTER_HIDDEN, 2 * D_HYENA), FILTER_HIDDEN ** -0.5),
        "hyena_bias": nrm(ks[17], (L_, D_HYENA), 1.0),
        "sconv_w": nrm(ks[18], (L_, CONV_WIDTH, D_SCONV), CONV_WIDTH ** -0.5),
        "q_norm_g": gain(ks[19], (L_, HEAD_DIM)),
        "k_norm_g": gain(ks[20], (L_, HEAD_DIM)),
        "attn_sink": nrm(ks[21], (L_, N_HEADS), 0.5),
        "mix_out_g": gain(ks[22], (L_, D_MIX)),
        "w_out": nrm(ks[23], (L_, D_MIX, D_MODEL), D_MIX ** -0.5),
        "norm_ffn2": gain(ks[24], (L_, D_MODEL)),
        "ffn2_w_gate": nrm(ks[25], (L_, D_MODEL, D_FF), D_MODEL ** -0.5),
        "ffn2_w_up": nrm(ks[26], (L_, D_MODEL, D_FF), D_MODEL ** -0.5),
        "ffn2_w_down": nrm(ks[27], (L_, D_FF, D_MODEL), D_FF ** -0.5),
    }


def reference(x, norm_ffn1, ffn1_w_gate, ffn1_w_up, ffn1_w_down, norm_mix, w_in,
              hyena_short_w, hyena_short_b, filt_w1, filt_b1, filt_w2, filt_b2,
              filt_w3, filt_b3, filt_freq, filt_w_out, hyena_bias, sconv_w,
              q_norm_g, k_norm_g, attn_sink, mix_out_g, w_out, norm_ffn2,
              ffn2_w_gate, ffn2_w_up, ffn2_w_down):
    for l in range(DEPTH):
        h = rms_norm(x, norm_ffn1[l])
        x = x + 0.5 * swiglu(h, ffn1_w_gate[l], ffn1_w_up[l], ffn1_w_down[l])

        h = rms_norm(x, norm_mix[l])
        proj = h @ w_in[l]
        p_hy = proj[..., :SPLIT_SC]
        p_sc = proj[..., SPLIT_SC:SPLIT_Q]
        q = proj[..., SPLIT_Q:SPLIT_K]
        k = proj[..., SPLIT_K:SPLIT_V]
        v = proj[..., SPLIT_V:]
        y_hy = hyena_mixer(p_hy, hyena_short_w[l], hyena_short_b[l], hyena_bias[l],
                           filt_w1[l], filt_b1[l], filt_w2[l], filt_b2[l],
                           filt_w3[l], filt_b3[l], filt_freq[l], filt_w_out[l])
        y_sc = short_conv_mixer(p_sc, sconv_w[l])
        y_at = windowed_gqa(q, k, v, q_norm_g[l], k_norm_g[l], attn_sink[l])
        g = mix_out_g[l]
        y = jnp.concatenate([
            rms_norm(y_hy, g[:D_HYENA]),
            rms_norm(y_sc, g[D_HYENA:D_HYENA + D_SCONV]),
            rms_norm(y_at, g[D_HYENA + D_SCONV:]),
        ], axis=-1)
        x = x + y @ w_out[l]

        h = rms_norm(x, norm_ffn2[l])
        x = x + 0.5 * swiglu(h, ffn2_w_gate[l], ffn2_w_up[l], ffn2_w_down[l])
    return x
```

```python
import math
from contextlib import ExitStack

import numpy as np
import ml_dtypes

import concourse.bass as bass
import concourse.mybir as mybir
from concourse.bass_utils import run_bass_kernel_spmd

F32 = mybir.dt.float32
BF16 = mybir.dt.bfloat16
ALU = mybir.AluOpType
AF = mybir.ActivationFunctionType

D = 2048
L = 4096
FF = 5632
NFC = FF // 128
SEGC = 4
NSEG = NFC // SEGC
TT = 1024
NTT = L // TT
EPS = 1e-6
NCORES = 4
DEPTH = 2
NPROJ = 36

C_GF1, C_GMIX, C_GF2 = 0, 16, 32
C_HSW, C_HSB, C_SCW, C_GHS = 48, 84, 96, 108
C_GQ, C_GK, C_FREQ, C_B1, C_B2, C_B3, C_SINK = 116, 117, 118, 119, 120, 121, 122
NSM = 138


class Buf:
    __slots__ = ("w", "r", "dsem", "dcnt", "name")

    def __init__(self, name=""):
        self.w = None
        self.r = {}
        self.dsem = None
        self.dcnt = 0
        self.name = name


class Eng:
    def __init__(self, name, eng, sem, is_pe=False):
        self.name = name
        self.eng = eng
        self.sem = sem
        self.n = 0
        self.seen = {}
        self.is_pe = is_pe


class Sched:
    def __init__(self, nc, stack):
        self.nc = nc
        self.stack = stack
        mk = lambda nm: stack.enter_context(nc.semaphore(nm))
        self.pe = Eng("pe", nc.tensor, mk("s_pe"), True)
        self.dve = Eng("dve", nc.vector, mk("s_dve"))
        self.act = Eng("act", nc.scalar, mk("s_act"))
        self.pool = Eng("pool", nc.gpsimd, mk("s_pool"))
        self.sp = Eng("sp", nc.sync, mk("s_sp"))
        self.engs = [self.pe, self.dve, self.act, self.pool, self.sp]
        self.dmabufs = []
        self.sempool = []
        self.nsem = 0

    def _deps(self, reads, writes):
        d = {}
        for b in reads:
            if b.w is not None:
                o, v = b.w
                if d.get(o, 0) < v:
                    d[o] = v
        for b in writes:
            if b.w is not None:
                o, v = b.w
                if d.get(o, 0) < v:
                    d[o] = v
            for o, v in b.r.items():
                if d.get(o, 0) < v:
                    d[o] = v
        return d

    def _wait(self, e, d):
        for o, v in d.items():
            if o is e and e.is_pe:
                continue
            if e.seen.get(o, 0) >= v:
                continue
            sem = o.sem if isinstance(o, Eng) else o.dsem
            e.eng.wait_ge(sem, v)
            e.seen[o] = v

    def _record(self, tick, reads, writes):
        o, v = tick
        for b in reads:
            if b.r.get(o, 0) < v:
                b.r[o] = v
        for b in writes:
            b.w = tick
            b.r = {}

    def op(self, e, fn, reads=(), writes=()):
        self._wait(e, self._deps(reads, writes))
        ins = fn(e.eng)
        e.n += 1
        ins.then_inc(e.sem, 1)
        self._record((e, e.n), reads, writes)

    def mm(self, outs, mms, reads, per=None):
        e = self.pe
        self._wait(e, self._deps(reads, outs))
        ins = None
        allr = list(reads)
        for i, (o, l, r, st, sp) in enumerate(mms):
            if per is not None:
                self._wait(e, self._deps(per[i], ()))
                allr.extend(per[i])
            ins = e.eng.matmul(o, lhsT=l, rhs=r, start=st, stop=sp)
        e.n += 1
        ins.then_inc(e.sem, 1)
        self._record((e, e.n), allr, outs)

    def tr(self, outs, trs, reads):
        e = self.pe
        self._wait(e, self._deps(reads, outs))
        ins = None
        for (o, i, idn) in trs:
            ins = e.eng.transpose(o, i, idn)
        e.n += 1
        ins.then_inc(e.sem, 1)
        self._record((e, e.n), reads, outs)

    def dma(self, q, out, in_, sb, reads=(), writes=()):
        if sb.dsem is None:
            if self.sempool:
                sb.dsem, sb.dcnt = self.sempool.pop()
            else:
                sb.dsem = self.stack.enter_context(self.nc.semaphore("d%d" % self.nsem))
                self.nsem += 1
            self.dmabufs.append(sb)
        d = self._deps(reads, writes)
        if sb.dcnt and d.get(sb, 0) < 16 * sb.dcnt:
            d[sb] = 16 * sb.dcnt
        self._wait(q, d)
        q.eng.dma_start(out=out, in_=in_).then_inc(sb.dsem, 16)
        sb.dcnt += 1
        self._record((sb, 16 * sb.dcnt), reads, writes)

    def barrier(self):
        for e in self.engs:
            for o in self.engs:
                if o is e or o.n == 0:
                    continue
                if e.seen.get(o, 0) < o.n:
                    e.eng.wait_ge(o.sem, o.n)
                    e.seen[o] = o.n
            for b in self.dmabufs:
                if b.dcnt and e.seen.get(b, 0) < 16 * b.dcnt:
                    e.eng.wait_ge(b.dsem, 16 * b.dcnt)
                    e.seen[b] = 16 * b.dcnt
        for b in self.dmabufs:
            for e in self.engs:
                e.seen[b] = 16 * b.dcnt
            self.sempool.append((b.dsem, b.dcnt))
            b.dsem = None
        self.dmabufs = []


def build_nc(stop_after=None, dbg=False):
    nc = bass.Bass("TRN2", target_bir_lowering=False)
    dt_in = lambda n, s, d=F32: nc.dram_tensor(n, list(s), d, kind="ExternalInput").ap()
    okind = "ExternalOutput" if dbg else "Internal"
    dt_sc = lambda n, s, d=F32: nc.dram_tensor(n, list(s), d, kind=okind).ap()

    xT = dt_in("xT", [D, L])
    W = []
    for l in range(DEPTH):
        w = {}
        for f in (1, 2):
            w["wg%d" % f] = dt_in("wg%d_%d" % (f, l), [NFC, 128, D])
            w["wu%d" % f] = dt_in("wu%d_%d" % (f, l), [NFC, 128, D])
            w["wd%d" % f] = dt_in("wd%d_%d" % (f, l), [FF, D])
        w["win"] = dt_in("win_%d" % l, [NPROJ, 128, D])
        w["winv"] = dt_in("winv_%d" % l, [128, 16 * 256])
        w["wout"] = dt_in("wout_%d" % l, [16, 128, D])
        w["sm"] = dt_in("sm_%d" % l, [128, NSM])
        w["gat"] = dt_in("gat_%d" % l, [128, 1024])
        w["hbias"] = dt_in("hbias_%d" % l, [1, 512])
        w["fw1"] = dt_in("fw1_%d" % l, [33, 64])
        w["fw2"] = dt_in("fw2_%d" % l, [64, 64])
        w["fw3"] = dt_in("fw3_%d" % l, [64, 64])
        w["fwo"] = dt_in("fwo_%d" % l, [64, 1024])
        W.append(w)
    zT_d = dt_in("zT", [33, L])
    winw_d = dt_in("winw", [L, 512])
    cfw_d = dt_in("cfw", [32, 128, 4096], BF16)
    sfw_d = dt_in("sfw", [32, 128, 4096], BF16)
    cinv_d = dt_in("cinv", [8, 4, 128, 4096], BF16)
    sinv_d = dt_in("sinv", [8, 4, 128, 4096], BF16)
    dmat_d = dt_in("dmat", [128, 384])
    identb_d = dt_in("identb", [128, 128], BF16)
    identf_d = dt_in("identf", [128, 128])

    yout = nc.dram_tensor("yout", [D, L], F32, kind="ExternalOutput").ap()
    xres = dt_sc("xres", [D, L])
    proj = dt_sc("proj", [NPROJ * 128, L])
    vtm = dt_sc("vtm", [L, 256])
    ymix = dt_sc("ymix", [D, L], BF16)
    ux0 = dt_sc("ux0", [512, L])
    kspec = dt_sc("kspec", [2, L, 512])
    qkn = dt_sc("qkn", [12 * 128, L], BF16)

    xres_b = [[Buf() for _ in range(16)] for _ in range(NTT)]
    proj_b = [Buf() for _ in range(NPROJ)]
    vtm_b = Buf()
    ymix_b = [Buf() for _ in range(16)]
    ux0_b = [Buf() for _ in range(4)]
    kspec_b = [Buf() for _ in range(32)]
    qkn_b = [Buf() for _ in range(12)]

    with ExitStack() as gs:
        S = Sched(nc, gs)
        V = lambda fn, r, w: S.op(S.dve, fn, r, w)
        A = lambda fn, r, w: S.op(S.act, fn, r, w)
        uniq = [0]

        def sb(st, n, s, d=F32):
            uniq[0] += 1
            return st.enter_context(nc.sbuf_tensor("%s_%d" % (n, uniq[0]), list(s), d))

        ps = gs.enter_context(nc.psum_tensor("ps", [128, 8, 512], F32))
        pb = [Buf("pb%d" % i) for i in range(8)]

        ones = sb(gs, "ones", [128, 128], BF16)
        blk64 = sb(gs, "blk64", [128, 128], BF16)
        identb = sb(gs, "identb_s", [128, 128], BF16)
        identf = sb(gs, "identf_s", [128, 128])
        negpi = sb(gs, "negpi", [128, 1])
        epst = sb(gs, "epst", [128, 1])
        zerot = sb(gs, "zerot", [128, 1])
        smt = [sb(gs, "smt%d" % l, [128, NSM]) for l in range(DEPTH)]
        cb = Buf("const")
        smb = [Buf(), Buf()]
        idb = [Buf(), Buf()]
        V(lambda e: e.memset(ones[:], 1.0), [], [cb])
        V(lambda e: e.memset(blk64[:], 0.0), [], [cb])
        V(lambda e: e.memset(blk64[0:64, 0:64], 1.0), [], [cb])
        V(lambda e: e.memset(blk64[64:128, 64:128], 1.0), [], [cb])
        V(lambda e: e.memset(negpi[:], -math.pi), [], [cb])
        V(lambda e: e.memset(epst[:], EPS), [], [cb])
        V(lambda e: e.memset(zerot[:], 0.0), [], [cb])

        def rsqrt(out, in_, scale, reads, wbuf):
            A(lambda e: e.activation(out=out, in_=in_, func=AF.Sqrt, bias=epst[:, 0:1], scale=scale), list(reads) + [cb], [wbuf])
            V(lambda e: e.reciprocal(out=out, in_=out), [wbuf], [wbuf])
        for l in range(DEPTH):
            S.dma(S.sp, smt[l][:], W[l]["sm"], smb[l], writes=[smb[l]])
        S.dma(S.sp, identb[:], identb_d, idb[0], writes=[idb[0]])
        S.dma(S.sp, identf[:], identf_d, idb[1], writes=[idb[1]])

        def tl_pass(l_prev, l_next, src, src_b, dst, dst_b):
            with ExitStack() as st:
                xt = sb(st, "xt", [128, 16, TT])
                hT = sb(st, "hT", [128, 16, TT], BF16)
                act = sb(st, "act", [128, SEGC, TT], BF16)
                wgt = [sb(st, "wgt%d" % i, [128, 16, 128], BF16) for i in range(2)]
                wut = [sb(st, "wut%d" % i, [128, 16, 128], BF16) for i in range(2)]
                wdt = [sb(st, "wdt%d" % i, [128, SEGC, 512], BF16) for i in range(2)]
                wvt = sb(st, "wvt", [128, 16, 256], BF16)
                silu = [sb(st, "silu%d" % i, [128, 512]) for i in range(2)]
                rstd = sb(st, "rstd", [128, TT])
                sq = [sb(st, "sq%d" % i, [128, TT], BF16) for i in range(2)]
                stg = [sb(st, "stg%d" % i, [128, TT]) for i in range(2)]
                vstg = sb(st, "vstg", [128, 8, 256])
                qstg = [sb(st, "qstg%d" % i, [128, TT], BF16) for i in range(2)]
                qsq = [sb(st, "qsq%d" % i, [128, 512], BF16) for i in range(2)]
                qrs = [sb(st, "qrs%d" % i, [128, 512]) for i in range(2)]
                qstgb = [Buf(), Buf()]
                qsqb = [Buf(), Buf()]
                qrsb = [Buf(), Buf()]
                xb = [Buf() for _ in range(16)]
                hb = [[Buf(), Buf()] for _ in range(16)]
                actb = [Buf() for _ in range(SEGC)]
                wgb = [Buf(), Buf()]
                wub = [Buf(), Buf()]
                wdb = [Buf(), Buf()]
                wvb = Buf()
                silub = [Buf(), Buf()]
                rstdb = [Buf(), Buf()]
                sqb = [Buf(), Buf()]
                stgb = [Buf(), Buf()]
                vstgb = Buf()
                cnt = {"w": 0, "wd": 0, "si": 0, "db": 0, "stg": 0, "bk": 0, "qs": 0, "qb": 0}

                def nxt(k, m=2):
                    v = cnt[k]
                    cnt[k] = (v + 1) % m
                    return v

                def norm(gcol):
                    for kc in range(16):
                        s_ = kc % 2
                        A(lambda e: e.activation(out=sq[s_][:], in_=xt[:, kc, :], func=AF.Square), [xb[kc]], [sqb[s_]])
                        for hf in range(2):
                            S.mm([pb[6 + hf]], [(ps[:, 6 + hf, :], ones[:], sq[s_][:, hf * 512:(hf + 1) * 512], kc == 0, kc == 15)], [sqb[s_], cb])
                    for hf in range(2):
                        sl = slice(hf * 512, (hf + 1) * 512)
                        rsqrt(rstd[:, sl], ps[:, 6 + hf, :], 1.0 / D, [pb[6 + hf]], rstdb[hf])
                        for kc in range(16):
                            V(lambda e: e.scalar_tensor_tensor(out=hT[:, kc, sl], in0=xt[:, kc, sl], scalar=gcol[:, kc:kc + 1], in1=rstd[:, sl], op0=ALU.mult, op1=ALU.mult), [xb[kc], rstdb[hf]], [hb[kc][hf]])

                def hper(hf):
                    return [[hb[kc][hf]] for kc in range(16)]

                def ffn(wg, wu, wd):
                    for seg in range(NSEG):
                        for cl in range(SEGC):
                            c = seg * SEGC + cl
                            s_ = nxt("w")
                            S.dma(S.pool, wgt[s_][:], wg[c].rearrange("p (k j) -> p k j", k=16), wgb[s_], writes=[wgb[s_]])
                            S.dma(S.pool, wut[s_][:], wu[c].rearrange("p (k j) -> p k j", k=16), wub[s_], writes=[wub[s_]])
                            for hf in range(2):
                                sl = slice(hf * 512, (hf + 1) * 512)
                                S.mm([pb[hf * 2]], [(ps[:, hf * 2, :], wgt[s_][:, kc, :], hT[:, kc, sl], kc == 0, kc == 15) for kc in range(16)], [wgb[s_]], per=hper(hf))
                                S.mm([pb[hf * 2 + 1]], [(ps[:, hf * 2 + 1, :], wut[s_][:, kc, :], hT[:, kc, sl], kc == 0, kc == 15) for kc in range(16)], [wub[s_]], per=hper(hf))
                                si = nxt("si")
                                A(lambda e: e.activation(out=silu[si][:], in_=ps[:, hf * 2, :], func=AF.Silu), [pb[hf * 2]], [silub[si]])
                                V(lambda e: e.tensor_tensor(out=act[:, cl, sl], in0=silu[si][:], in1=ps[:, hf * 2 + 1, :], op=ALU.mult), [silub[si], pb[hf * 2 + 1]], [actb[cl]])
                        for dq in range(4):
                            s_ = nxt("wd")
                            S.dma(S.pool, wdt[s_][:], wd[seg * 512:(seg + 1) * 512, dq * 512:(dq + 1) * 512].rearrange("(c p) n -> p c n", p=128), wdb[s_], writes=[wdb[s_]])
                            for dd in range(4):
                                d = dq * 4 + dd
                                for hf in range(2):
                                    sl = slice(hf * 512, (hf + 1) * 512)
                                    bk = 4 + nxt("db", 4)
                                    S.mm([pb[bk]], [(ps[:, bk, :], wdt[s_][:, cl, dd * 128:(dd + 1) * 128], act[:, cl, sl], cl == 0, cl == SEGC - 1) for cl in range(SEGC)], [wdb[s_]] + actb)
                                    V(lambda e: e.scalar_tensor_tensor(out=xt[:, d, sl], in0=ps[:, bk, :], scalar=0.5, in1=xt[:, d, sl], op0=ALU.mult, op1=ALU.add), [pb[bk], xb[d]], [xb[d]])

                def inproj(w, tt, w_sm, w_smb):
                    tsl = slice(tt * TT, (tt + 1) * TT)
                    for j in range(NPROJ):
                        s_ = nxt("w")
                        S.dma(S.pool, wgt[s_][:], w["win"][j].rearrange("p (k j) -> p k j", k=16), wgb[s_], writes=[wgb[s_]])
                        g_ = nxt("stg")
                        for hf in range(2):
                            sl = slice(hf * 512, (hf + 1) * 512)
                            bk = nxt("bk", 4)
                            S.mm([pb[bk]], [(ps[:, bk, :], wgt[s_][:, kc, :], hT[:, kc, sl], kc == 0, kc == 15) for kc in range(16)], [wgb[s_]], per=hper(hf))
                            if j < 24:
                                A(lambda e: e.copy(out=stg[g_][:, sl], in_=ps[:, bk, :]), [pb[bk]], [stgb[g_]])
                            else:
                                q_ = nxt("qs")
                                nb = 4 + nxt("qb", 4)
                                gcol = w_sm[:, C_GQ:C_GQ + 1] if j < 32 else w_sm[:, C_GK:C_GK + 1]
                                A(lambda e: e.activation(out=qsq[q_][:], in_=ps[:, bk, :], func=AF.Square), [pb[bk]], [qsqb[q_]])
                                S.mm([pb[nb]], [(ps[:, nb, :], blk64[:], qsq[q_][:], True, True)], [qsqb[q_], cb])
                                A(lambda e: e.activation(out=qrs[q_][:], in_=ps[:, nb, :], func=AF.Ln, bias=epst[:, 0:1], scale=1.0 / 64), [pb[nb], cb], [qrsb[q_]])
                                A(lambda e: e.activation(out=qrs[q_][:], in_=qrs[q_][:], func=AF.Exp, scale=-0.5), [qrsb[q_]], [qrsb[q_]])
                                V(lambda e: e.scalar_tensor_tensor(out=qstg[g_][:, sl], in0=ps[:, bk, :], scalar=gcol, in1=qrs[q_][:], op0=ALU.mult, op1=ALU.mult), [pb[bk], qrsb[q_], w_smb], [qstgb[g_]])
                        if j < 24:
                            S.dma(S.act, proj[j * 128:(j + 1) * 128, tsl], stg[g_][:], stgb[g_], reads=[stgb[g_]], writes=[proj_b[j]])
                        else:
                            S.dma(S.act, qkn[(j - 24) * 128:(j - 23) * 128, tsl], qstg[g_][:], qstgb[g_], reads=[qstgb[g_]], writes=[qkn_b[j - 24]])
                    S.dma(S.pool, wvt[:], w["winv"].rearrange("p (k j) -> p k j", k=16), wvb, writes=[wvb])
                    for blk in range(8):
                        bk = nxt("bk", 4)
                        S.mm([pb[bk]], [(ps[:, bk, 0:256], hT[:, kc, blk * 128:(blk + 1) * 128], wvt[:, kc, :], kc == 0, kc == 15) for kc in range(16)], [wvb], per=hper(blk // 4))
                        A(lambda e: e.copy(out=vstg[:, blk, :], in_=ps[:, bk, 0:256]), [pb[bk]], [vstgb])
                    S.dma(S.act, vtm[tsl, :].rearrange("(b p) c -> p b c", p=128), vstg[:], vstgb, reads=[vstgb], writes=[vtm_b])

                def outproj(w, tt):
                    tsl = slice(tt * TT, (tt + 1) * TT)
                    for kc in range(16):
                        S.dma(S.sp, hT[:, kc, :], ymix[kc * 128:(kc + 1) * 128, tsl], hb[kc][0], reads=[ymix_b[kc]], writes=[hb[kc][0], hb[kc][1]])
                    for d in range(16):
                        s_ = nxt("w")
                        S.dma(S.pool, wgt[s_][:], w["wout"][d].rearrange("p (k j) -> p k j", k=16), wgb[s_], writes=[wgb[s_]])
                        for hf in range(2):
                            sl = slice(hf * 512, (hf + 1) * 512)
                            bk = 4 + nxt("db", 4)
                            S.mm([pb[bk]], [(ps[:, bk, :], wgt[s_][:, kc, :], hT[:, kc, sl], kc == 0, kc == 15) for kc in range(16)], [wgb[s_]], per=hper(hf))
                            V(lambda e: e.tensor_tensor(out=xt[:, d, sl], in0=ps[:, bk, :], in1=xt[:, d, sl], op=ALU.add), [pb[bk], xb[d]], [xb[d]])

                def xload(tt):
                    tsl = slice(tt * TT, (tt + 1) * TT)
                    for d in range(16):
                        S.dma(S.sp, xt[:, d, :], src[d * 128:(d + 1) * 128, tsl], xb[d], reads=([src_b[tt][d]] if src_b else []), writes=[xb[d]])

                def xstore(tt):
                    tsl = slice(tt * TT, (tt + 1) * TT)
                    for d in range(16):
                        S.dma(S.sp, dst[d * 128:(d + 1) * 128, tsl], xt[:, d, :], xb[d], reads=[xb[d]], writes=([dst_b[tt][d]] if dst_b else []))

                xload(0)
                for tt in range(NTT):
                    if l_prev is not None:
                        w = W[l_prev]
                        outproj(w, tt)
                        norm(smt[l_prev][:, C_GF2:C_GF2 + 16])
                        ffn(w["wg2"], w["wu2"], w["wd2"])
                    if l_next is not None:
                        w = W[l_next]
                        norm(smt[l_next][:, C_GF1:C_GF1 + 16])
                        ffn(w["wg1"], w["wu1"], w["wd1"])
                        norm(smt[l_next][:, C_GMIX:C_GMIX + 16])
                    xstore(tt)
                    if tt + 1 < NTT:
                        xload(tt + 1)
                    if l_next is not None:
                        inproj(W[l_next], tt, smt[l_next], smb[l_next])
                S.barrier()

        def conv3(pad, u, padb, ub, w0, w1, w2, bias, n=L):
            P = lambda fn, r, w: S.op(S.pool, fn, r, w)
            A(lambda e: e.activation(out=u[:, 0:n], in_=pad[:, 1:n + 1], func=AF.Identity, bias=(bias if bias is not None else zerot[:, 0:1]), scale=w1), [padb, cb], [ub])
            V(lambda e: e.scalar_tensor_tensor(out=u[:, 0:n], in0=pad[:, 0:n], scalar=w0, in1=u[:, 0:n], op0=ALU.mult, op1=ALU.add), [padb, ub], [ub])
            V(lambda e: e.scalar_tensor_tensor(out=u[:, 0:n], in0=pad[:, 2:n + 2], scalar=w2, in1=u[:, 0:n], op0=ALU.mult, op1=ALU.add), [padb, ub], [ub])

        def mx_filter(l):
            w = W[l]
            sm = smt[l]
            with ExitStack() as sk:
              kp = sb(sk, "kp", [128, 32, 512], BF16)
              km = sb(sk, "km", [128, 32, 512], BF16)
              kpb, kmb = Buf(), Buf()
              with ExitStack() as st:
                zt = sb(st, "zt", [64, L])
                hA = sb(st, "hA", [64, L])
                hB = sb(st, "hB", [64, L])
                ut = [sb(st, "ut%d" % i, [64, 512]) for i in range(2)]
                ut2 = [sb(st, "utb%d" % i, [64, 512]) for i in range(2)]
                ut2b = [Buf(), Buf()]
                w1t = sb(st, "w1t", [64, 64])
                w2t = sb(st, "w2t", [64, 64])
                w3t = sb(st, "w3t", [64, 64])
                wot = sb(st, "wot", [64, 1024])
                s1 = sb(st, "s1", [64, 4])
                s2 = sb(st, "s2", [64, 4])
                hbt = sb(st, "hbt", [1, 512])
                hbs = [sb(st, "hbs%d" % i, [128, 512]) for i in range(2)]
                t1 = [sb(st, "t1%d" % i, [128, 512]) for i in range(2)]
                t2 = [sb(st, "t2%d" % i, [128, 512]) for i in range(2)]
                wint = [sb(st, "wint%d" % i, [128, 512]) for i in range(2)]
                ztb, hAb, hBb = Buf(), Buf(), Buf()
                utb = [Buf(), Buf()]
                wb = [Buf() for _ in range(5)]
                sb12 = Buf()
                hbsb = [Buf(), Buf()]
                t1b = [Buf(), Buf()]
                t2b = [Buf(), Buf()]
                wintb = [Buf(), Buf()]
                S.dma(S.sp, zt[0:33, :], zT_d, ztb, writes=[ztb])
                S.dma(S.sp, w1t[0:33, :], w["fw1"], wb[0], writes=[wb[0]])
                S.dma(S.sp, w2t[:], w["fw2"], wb[1], writes=[wb[1]])
                S.dma(S.sp, w3t[:], w["fw3"], wb[2], writes=[wb[2]])
                S.dma(S.sp, wot[:], w["fwo"], wb[3], writes=[wb[3]])
                S.dma(S.sp, hbt[:], w["hbias"], wb[4], writes=[wb[4]])
                V(lambda e: e.tensor_scalar(out=s1[:, 0:1], in0=sm[0:64, C_FREQ:C_FREQ + 1], scalar1=1.0 / (2 * math.pi), scalar2=None, op0=ALU.mult), [smb[l]], [sb12])
                for i in range(3):
                    V(lambda e: e.tensor_scalar(out=s2[:, i:i + 1], in0=sm[0:64, C_B1 + i:C_B1 + i + 1], scalar1=s1[:, 0:1], scalar2=None, op0=ALU.mult), [smb[l], sb12], [sb12])
                layers = [(w1t, 33, zt, ztb, hA, hAb, wb[0]), (w2t, 64, hA, hAb, hB, hBb, wb[1]), (w3t, 64, hB, hBb, hA, hAb, wb[2])]
                bkc = [0]
                for i, (wt, kd, xin, xinb, hout, houtb, wtb) in enumerate(layers):
                    for n in range(8):
                        sl = slice(n * 512, (n + 1) * 512)
                        bk = bkc[0]
                        bkc[0] = (bk + 1) % 4
                        S.mm([pb[bk]], [(ps[0:64, bk, :], wt[0:kd, :], xin[0:kd, sl], True, True)], [wtb, xinb])
                        u_ = n % 2
                        V(lambda e: e.tensor_scalar(out=ut[u_][:], in0=ps[0:64, bk, :], scalar1=s1[:, 0:1], scalar2=s2[:, i:i + 1], op0=ALU.mult, op1=ALU.add), [pb[bk], sb12], [utb[u_]])
                        V(lambda e: e.tensor_scalar(out=ut2[u_][:], in0=ut[u_][:], scalar1=12582912.0, scalar2=None, op0=ALU.add), [utb[u_]], [ut2b[u_]])
                        V(lambda e: e.tensor_scalar(out=ut2[u_][:], in0=ut2[u_][:], scalar1=-12582912.0, scalar2=None, op0=ALU.add), [ut2b[u_]], [ut2b[u_]])
                        V(lambda e: e.tensor_tensor(out=ut[u_][:], in0=ut[u_][:], in1=ut2[u_][:], op=ALU.subtract), [utb[u_], ut2b[u_]], [utb[u_]])
                        A(lambda e: e.activation(out=hout[:, sl], in_=ut[u_][:], func=AF.Sin, scale=2 * math.pi), [utb[u_], cb], [houtb])
                h3, h3b = hA, hAb
                for nb in range(32):
                    i_ = nb % 2
                    b0, b1 = (0, 1) if i_ == 0 else (2, 3)
                    S.mm([pb[b0]], [(ps[:, b0, :], h3[:, nb * 128:(nb + 1) * 128], wot[:, 0:512], True, True)], [h3b, wb[3]])
                    S.mm([pb[b1]], [(ps[:, b1, :], h3[:, nb * 128:(nb + 1) * 128], wot[:, 512:1024], True, True)], [h3b, wb[3]])
                    S.dma(S.sp, wint[i_][:], winw_d[nb * 128:(nb + 1) * 128, :], wintb[i_], writes=[wintb[i_]])
                    A(lambda e: e.copy(out=hbs[i_][:], in_=ps[:, b1, :]), [pb[b1]], [hbsb[i_]])
                    if nb == 0:
                        V(lambda e: e.memset(hbs[i_][0:1, :], 0.0), [], [hbsb[i_]])
                    V(lambda e: e.tensor_tensor(out=t1[i_][:], in0=ps[:, b0, :], in1=hbs[i_][:], op=ALU.add), [pb[b0], hbsb[i_]], [t1b[i_]])
                    V(lambda e: e.tensor_tensor(out=t2[i_][:], in0=hbs[i_][:], in1=ps[:, b0, :], op=ALU.subtract), [pb[b0], hbsb[i_]], [t2b[i_]])
                    if nb == 0:
                        V(lambda e: e.tensor_tensor(out=t1[i_][:], in0=t1[i_][:], in1=wint[i_][:], op=ALU.mult), [t1b[i_], wintb[i_]], [t1b[i_]])
                        V(lambda e: e.tensor_tensor(out=t2[i_][:], in0=t2[i_][:], in1=wint[i_][:], op=ALU.mult), [t2b[i_], wintb[i_]], [t2b[i_]])
                        V(lambda e: e.tensor_tensor(out=t1[i_][0:1, :], in0=t1[i_][0:1, :], in1=hbt[0:1, :], op=ALU.add), [t1b[i_], wb[4]], [t1b[i_]])
                        V(lambda e: e.tensor_tensor(out=t2[i_][0:1, :], in0=t2[i_][0:1, :], in1=hbt[0:1, :], op=ALU.subtract), [t2b[i_], wb[4]], [t2b[i_]])
                        V(lambda e: e.tensor_copy(out=kp[:, nb, :], in_=t1[i_][:]), [t1b[i_]], [kpb])
                        V(lambda e: e.tensor_copy(out=km[:, nb, :], in_=t2[i_][:]), [t2b[i_]], [kmb])
                    else:
                        V(lambda e: e.tensor_tensor(out=kp[:, nb, :], in0=t1[i_][:], in1=wint[i_][:], op=ALU.mult), [t1b[i_], wintb[i_]], [kpb])
                        V(lambda e: e.tensor_tensor(out=km[:, nb, :], in0=t2[i_][:], in1=wint[i_][:], op=ALU.mult), [t2b[i_], wintb[i_]], [kmb])
                S.barrier()
              with ExitStack() as st:
                ct = [sb(st, "fct%d" % i, [128, 16, 128], BF16) for i in range(4)]
                stl = [sb(st, "fst%d" % i, [128, 16, 128], BF16) for i in range(4)]
                kstg = [sb(st, "kstg%d" % i, [128, 2, 512]) for i in range(2)]
                ctb = [Buf() for _ in range(4)]
                stb = [Buf() for _ in range(4)]
                kstgb = [Buf(), Buf()]
                gen = sconv_units(l, st)
                for fch in range(32):
                    s_ = fch % 2
                    sl_ = [(2 * fch + hh) % 4 for hh in range(2)]
                    for hh in range(2):
                        S.dma(S.sp, ct[sl_[hh]][:], cfw_d[fch][:, hh * 2048:(hh + 1) * 2048].rearrange("p (k j) -> p k j", k=16), ctb[sl_[hh]], writes=[ctb[sl_[hh]]])
                        S.dma(S.sp, stl[sl_[hh]][:], sfw_d[fch][:, hh * 2048:(hh + 1) * 2048].rearrange("p (k j) -> p k j", k=16), stb[sl_[hh]], writes=[stb[sl_[hh]]])
                    b0, b1 = (4, 5) if s_ == 0 else (6, 7)
                    S.mm([pb[b0]], [(ps[:, b0, :], ct[sl_[tc // 16]][:, tc % 16, :], kp[:, tc, :], tc == 0, tc == 31) for tc in range(32)], [kpb], per=[[ctb[sl_[tc // 16]]] for tc in range(32)])
                    S.mm([pb[b1]], [(ps[:, b1, :], stl[sl_[tc // 16]][:, tc % 16, :], km[:, tc, :], tc == 0, tc == 31) for tc in range(32)], [kmb], per=[[stb[sl_[tc // 16]]] for tc in range(32)])
                    A(lambda e: e.activation(out=kstg[s_][:, 0, :], in_=ps[:, b0, :], func=AF.Copy, scale=2.0 / 8192.0), [pb[b0]], [kstgb[s_]])
                    A(lambda e: e.activation(out=kstg[s_][:, 1, :], in_=ps[:, b1, :], func=AF.Copy, scale=2.0 / 8192.0), [pb[b1]], [kstgb[s_]])
                    S.dma(S.act, kspec[:, fch * 128:(fch + 1) * 128, :].rearrange("a p c -> p a c"), kstg[s_][:], kstgb[s_], reads=[kstgb[s_]], writes=[kspec_b[fch]])
                    if fch % 2 == 1:
                        next(gen, None)
                for _ in gen:
                    pass
                S.barrier()

        def mx_hyena(l):
            sm = smt[l]
            with ExitStack() as sz:
                zvT = sb(sz, "zvT", [128, 32, 512], BF16)
                zvTb = Buf()
                with ExitStack() as st:
                    pads = [sb(st, "hpad%d" % i, [128, L + 2]) for i in range(4)]
                    us = [sb(st, "hu%d" % i, [128, L]) for i in range(3)]
                    padb = [Buf() for _ in range(4)]
                    ub = [Buf() for _ in range(3)]
                    for i in range(4):
                        V(lambda e: e.memset(pads[i][:, 0:1], 0.0), [], [padb[i]])
                        V(lambda e: e.memset(pads[i][:, L + 1:L + 2], 0.0), [], [padb[i]])

                    def taps(j):
                        c0 = C_HSW + j * 3
                        return sm[:, c0:c0 + 1], sm[:, c0 + 1:c0 + 2], sm[:, c0 + 2:c0 + 3], sm[:, C_HSB + j:C_HSB + j + 1]

                    k_ = 0
                    for j in range(4):
                        pis = []
                        for part in range(3):
                            pi = k_ % 4
                            k_ += 1
                            pis.append(pi)
                            pj = part * 4 + j
                            S.dma(S.sp, pads[pi][:, 1:L + 1], proj[pj * 128:(pj + 1) * 128, :], padb[pi], reads=[proj_b[pj]], writes=[padb[pi]])
                        for part in range(3):
                            pi = pis[part]
                            pj = part * 4 + j
                            w0, w1, w2, bb = taps(pj)
                            conv3(pads[pi], us[part], padb[pi], ub[part], w0, w1, w2, bb)
                        S.dma(S.act, ux0[j * 128:(j + 1) * 128, :], us[0][:], ub[0], reads=[ub[0]], writes=[ux0_b[j]])
                        uA, uB, uAb, uBb = us[1], us[2], ub[1], ub[2]
                        V(lambda e: e.tensor_tensor(out=uB[:], in0=uB[:], in1=uA[:], op=ALU.mult), [uAb, uBb], [uBb])
                        for b4 in range(8):
                            bk = b4 % 4
                            S.tr([pb[bk]], [(ps[:, bk, q * 128:(q + 1) * 128], uB[:, (b4 * 4 + q) * 128:(b4 * 4 + q + 1) * 128], identf[:]) for q in range(4)], [uBb, idb[1]])
                            A(lambda e: e.copy(out=zvT[:, b4 * 4:(b4 + 1) * 4, j * 128:(j + 1) * 128], in_=ps[:, bk, :].rearrange("p (q c) -> p q c", q=4)), [pb[bk]], [zvTb])
                    S.barrier()
                with ExitStack() as sy:
                    Yre = sb(sy, "Yre", [128, 32, 512], BF16)
                    Yim = sb(sy, "Yim", [128, 32, 512], BF16)
                    Yreb, Yimb = Buf(), Buf()
                    with ExitStack() as st:
                        ct = [sb(st, "zct%d" % i, [128, 16, 128], BF16) for i in range(4)]
                        stl = [sb(st, "zst%d" % i, [128, 16, 128], BF16) for i in range(4)]
                        kld = [sb(st, "kld%d" % i, [128, 2, 512]) for i in range(2)]
                        ta = [sb(st, "ta%d" % i, [128, 512]) for i in range(4)]
                        ctb = [Buf() for _ in range(4)]
                        stb = [Buf() for _ in range(4)]
                        kldb = [Buf(), Buf()]
                        tab = [Buf() for _ in range(4)]
                        for fch in range(32):
                            s_ = fch % 2
                            sl_ = [(2 * fch + hh) % 4 for hh in range(2)]
                            for hh in range(2):
                                S.dma(S.sp, ct[sl_[hh]][:], cfw_d[fch][:, hh * 2048:(hh + 1) * 2048].rearrange("p (k j) -> p k j", k=16), ctb[sl_[hh]], writes=[ctb[sl_[hh]]])
                                S.dma(S.sp, stl[sl_[hh]][:], sfw_d[fch][:, hh * 2048:(hh + 1) * 2048].rearrange("p (k j) -> p k j", k=16), stb[sl_[hh]], writes=[stb[sl_[hh]]])
                            S.dma(S.act, kld[s_][:], kspec[:, fch * 128:(fch + 1) * 128, :].rearrange("a p c -> p a c"), kldb[s_], reads=[kspec_b[fch]], writes=[kldb[s_]])
                            b0, b1 = (0, 1) if s_ == 0 else (2, 3)
                            S.mm([pb[b0]], [(ps[:, b0, :], ct[sl_[tc // 16]][:, tc % 16, :], zvT[:, tc, :], tc == 0, tc == 31) for tc in range(32)], [zvTb], per=[[ctb[sl_[tc // 16]]] for tc in range(32)])
                            S.mm([pb[b1]], [(ps[:, b1, :], stl[sl_[tc // 16]][:, tc % 16, :], zvT[:, tc, :], tc == 0, tc == 31) for tc in range(32)], [zvTb], per=[[stb[sl_[tc // 16]]] for tc in range(32)])
                            kre, kim = kld[s_][:, 0, :], kld[s_][:, 1, :]
                            V(lambda e: e.tensor_tensor(out=ta[0][:], in0=ps[:, b0, :], in1=kre, op=ALU.mult), [pb[b0], kldb[s_]], [tab[0]])
                            V(lambda e: e.tensor_tensor(out=ta[1][:], in0=ps[:, b1, :], in1=kim, op=ALU.mult), [pb[b1], kldb[s_]], [tab[1]])
                            V(lambda e: e.tensor_tensor(out=Yre[:, fch, :], in0=ta[0][:], in1=ta[1][:], op=ALU.add), [tab[0], tab[1]], [Yreb])
                            V(lambda e: e.tensor_tensor(out=ta[2][:], in0=ps[:, b0, :], in1=kim, op=ALU.mult), [pb[b0], kldb[s_]], [tab[2]])
                            V(lambda e: e.tensor_tensor(out=ta[3][:], in0=ps[:, b1, :], in1=kre, op=ALU.mult), [pb[b1], kldb[s_]], [tab[3]])
                            V(lambda e: e.tensor_tensor(out=Yim[:, fch, :], in0=ta[2][:], in1=ta[3][:], op=ALU.subtract), [tab[2], tab[3]], [Yimb])
                        S.barrier()
                    with ExitStack() as st:
                        cit = [sb(st, "cit%d" % i, [128, 4, 512], BF16) for i in range(4)]
                        sit = [sb(st, "sit%d" % i, [128, 4, 512], BF16) for i in range(4)]
                        uxt = sb(st, "uxt", [128, 4, 512])
                        yh = sb(st, "yh", [128, 4, 512])
                        sq = [sb(st, "hsq%d" % i, [128, 512], BF16) for i in range(2)]
                        rs = sb(st, "hrs", [128, 512])
                        yo = [sb(st, "hyo%d" % i, [128, 4, 512], BF16) for i in range(2)]
                        citb = [Buf() for _ in range(4)]
                        sitb = [Buf() for _ in range(4)]
                        uxb, rsb = Buf(), Buf()
                        yhb = [Buf() for _ in range(4)]
                        sqb = [Buf(), Buf()]
                        yob = [Buf(), Buf()]
                        k_ = 0
                        for tt in range(8):
                            tsl = slice(tt * 512, (tt + 1) * 512)
                            for fg8 in range(8):
                                s_ = k_ % 4
                                k_ += 1
                                fg, hh = fg8 // 2, fg8 % 2
                                S.dma(S.sp, cit[s_][:], cinv_d[tt, fg][:, hh * 2048:(hh + 1) * 2048].rearrange("p (k t) -> p k t", k=4), citb[s_], writes=[citb[s_]])
                                S.dma(S.sp, sit[s_][:], sinv_d[tt, fg][:, hh * 2048:(hh + 1) * 2048].rearrange("p (k t) -> p k t", k=4), sitb[s_], writes=[sitb[s_]])
                                for cc in range(4):
                                    mms = []
                                    for fc in range(4):
                                        f = fg8 * 4 + fc
                                        mms.append((ps[:, cc, :], Yre[:, f, cc * 128:(cc + 1) * 128], cit[s_][:, fc, :], fg8 == 0 and fc == 0, False))
                                        mms.append((ps[:, cc, :], Yim[:, f, cc * 128:(cc + 1) * 128], sit[s_][:, fc, :], False, fg8 == 7 and fc == 3))
                                    S.mm([pb[cc]], mms, [citb[s_], sitb[s_], Yreb, Yimb])
                            S.dma(S.act, uxt[:], ux0[:, tsl].rearrange("(c p) t -> p c t", p=128), uxb, reads=ux0_b, writes=[uxb])
                            nbk = 4 + tt % 2
                            for cc in range(4):
                                V(lambda e: e.tensor_tensor(out=yh[:, cc, :], in0=ps[:, cc, :], in1=uxt[:, cc, :], op=ALU.mult), [pb[cc], uxb], [yhb[cc]])
                                A(lambda e: e.activation(out=sq[cc % 2][:], in_=yh[:, cc, :], func=AF.Square), [yhb[cc]], [sqb[cc % 2]])
                                S.mm([pb[nbk]], [(ps[:, nbk, :], ones[:], sq[cc % 2][:], cc == 0, cc == 3)], [sqb[cc % 2], cb])
                            rsqrt(rs[:], ps[:, nbk, :], 1.0 / 512, [pb[nbk]], rsb)
                            o_ = tt % 2
                            for cc in range(4):
                                V(lambda e: e.scalar_tensor_tensor(out=yo[o_][:, cc, :], in0=yh[:, cc, :], scalar=sm[:, C_GHS + cc:C_GHS + cc + 1], in1=rs[:], op0=ALU.mult, op1=ALU.mult), [yhb[cc], rsb, smb[l]], [yob[o_]])
                            S.dma(S.act, ymix[0:512, tsl].rearrange("(c p) t -> p c t", p=128), yo[o_][:], yob[o_], reads=[yob[o_]], writes=ymix_b[0:4])
                        S.barrier()

        def sconv_units(l, st):
            sm = smt[l]
            H = L // 2
            ysc = [sb(st, "ysc%d" % i, [128, H]) for i in range(4)]
            pad = [sb(st, "scpad%d" % i, [128, H + 2]) for i in range(2)]
            ldH = [sb(st, "ldH%d" % i, [128, H + 2]) for i in range(2)]
            sq = [sb(st, "ssq%d" % i, [128, 512], BF16) for i in range(2)]
            rs = sb(st, "srs", [128, 512])
            yo = [sb(st, "syo%d" % i, [128, 4, 512], BF16) for i in range(2)]
            rsb = Buf()
            padb = [Buf(), Buf()]
            ldHb = [Buf(), Buf()]
            yscb = [Buf() for _ in range(4)]
            sqb = [Buf(), Buf()]
            yob = [Buf(), Buf()]

            def loads(hh, j):
                s_ = j % 2
                t0 = hh * H
                S.dma(S.pool, ysc[j][:], proj[(12 + j) * 128:(13 + j) * 128, t0:t0 + H], yscb[j], reads=[proj_b[12 + j]], writes=[yscb[j]])
                for (buf, bufb, ch) in ((pad[s_], padb[s_], 16 + j), (ldH[s_], ldHb[s_], 20 + j)):
                    if hh == 0:
                        V(lambda e: e.memset(buf[:, 0:1], 0.0), [], [bufb])
                        S.dma(S.pool, buf[:, 1:H + 2], proj[ch * 128:(ch + 1) * 128, 0:H + 1], bufb, reads=[proj_b[ch]], writes=[bufb])
                    else:
                        V(lambda e: e.memset(buf[:, H + 1:H + 2], 0.0), [], [bufb])
                        S.dma(S.pool, buf[:, 0:H + 1], proj[ch * 128:(ch + 1) * 128, H - 1:L], bufb, reads=[proj_b[ch]], writes=[bufb])

            for hh in range(2):
                loads(hh, 0)
                for j in range(4):
                    s_ = j % 2
                    if j + 1 < 4:
                        loads(hh, j + 1)
                    V(lambda e: e.tensor_tensor(out=pad[s_][:], in0=pad[s_][:], in1=ldH[s_][:], op=ALU.mult), [padb[s_], ldHb[s_]], [padb[s_]])
                    c0 = C_SCW + j * 3
                    conv3(pad[s_], ldH[s_], padb[s_], ldHb[s_], sm[:, c0:c0 + 1], sm[:, c0 + 1:c0 + 2], sm[:, c0 + 2:c0 + 3], None, n=H)
                    V(lambda e: e.tensor_tensor(out=ysc[j][:], in0=ysc[j][:], in1=ldH[s_][:, 0:H], op=ALU.mult), [yscb[j], ldHb[s_]], [yscb[j]])
                    yield
                for t4 in range(4):
                    tt = hh * 4 + t4
                    lsl = slice(t4 * 512, (t4 + 1) * 512)
                    tsl = slice(tt * 512, (tt + 1) * 512)
                    nbk = t4 % 2
                    for cc in range(4):
                        A(lambda e: e.activation(out=sq[cc % 2][:], in_=ysc[cc][:, lsl], func=AF.Square), [yscb[cc]], [sqb[cc % 2]])
                        S.mm([pb[nbk]], [(ps[:, nbk, :], ones[:], sq[cc % 2][:], cc == 0, cc == 3)], [sqb[cc % 2], cb])
                    rsqrt(rs[:], ps[:, nbk, :], 1.0 / 512, [pb[nbk]], rsb)
                    o_ = tt % 2
                    for cc in range(4):
                        V(lambda e: e.scalar_tensor_tensor(out=yo[o_][:, cc, :], in0=ysc[cc][:, lsl], scalar=sm[:, C_GHS + 4 + cc:C_GHS + 5 + cc], in1=rs[:], op0=ALU.mult, op1=ALU.mult), [yscb[cc], rsb, smb[l]], [yob[o_]])
                    S.dma(S.pool, ymix[512:1024, tsl].rearrange("(c p) t -> p c t", p=128), yo[o_][:], yob[o_], reads=[yob[o_]], writes=ymix_b[4:8])
                    yield

        def mx_attn(l):
            sm = smt[l]
            w = W[l]
            VA = 68
            with ExitStack() as st:
                qn = sb(st, "qn", [128, 8, L], BF16)
                kd = sb(st, "kd", [128, 4, L], BF16)
                vaug = sb(st, "vaug", [128, 32, 4, VA], BF16)
                ld = [sb(st, "ald%d" % i, [128, 1024]) for i in range(2)]
                sqa = [sb(st, "asq%d" % i, [128, 1024], BF16) for i in range(2)]
                rs = sb(st, "ars", [128, 1024])
                gat = sb(st, "gatt", [128, 1024])
                dm = sb(st, "dmt", [128, 384])
                tmp = [sb(st, "atmp%d" % i, [128, 384]) for i in range(8)]
                pt = [sb(st, "apt%d" % i, [128, 384], BF16) for i in range(8)]
                yat = [sb(st, "yat%d" % i, [128, 1024]) for i in range(2)]
                ybf = [sb(st, "ybf%d" % i, [128, 1024], BF16) for i in range(2)]
                junk = sb(st, "ajunk", [128, 1024], BF16)
                yts = [sb(st, "yts%d" % i, [128, 8, 512], BF16) for i in range(2)]
                esink = sb(st, "esink", [128, 16])
                den = sb(st, "aden", [128, 8])
                rec = sb(st, "arec", [128, 8])
                ssq = sb(st, "assq", [128, 2])
                qnb, kdb, vaugb, rsb, gatb, dmb, junkb, esb = [Buf() for _ in range(8)]
                denb = [Buf(), Buf()]
                recb = [Buf(), Buf()]
                ssqb = [Buf(), Buf()]
                ldb = [Buf(), Buf()]
                sqab = [Buf(), Buf()]
                tmpb = [Buf() for _ in range(8)]
                ptb = [Buf() for _ in range(8)]
                yatb = [[Buf() for _ in range(4)] for _ in range(2)]
                ybfb = [Buf(), Buf()]
                ytsb = [Buf(), Buf()]
                S.dma(S.sp, gat[:], w["gat"], gatb, writes=[gatb])
                S.dma(S.sp, dm[:], dmat_d, dmb, writes=[dmb])
                A(lambda e: e.activation(out=esink[:], in_=sm[:, C_SINK:C_SINK + 16], func=AF.Exp), [smb[l]], [esb])
                V(lambda e: e.memset(vaug[:, :, :, 64:65], 1.0), [], [vaugb])
                k_ = 0
                qcb = [Buf() for _ in range(12)]
                for ch in range(12):
                    dst = qn if ch < 8 else kd
                    dstb = qnb if ch < 8 else kdb
                    dch = ch if ch < 8 else ch - 8
                    S.dma(S.sp, dst[:, dch, :], qkn[ch * 128:(ch + 1) * 128, :], qcb[ch], reads=[qkn_b[ch]], writes=[qcb[ch]])
                for pc in range(8):
                    s_ = k_ % 2
                    k_ += 1
                    S.dma(S.sp, ld[s_][:].rearrange("p (b c) -> p b c", b=4), vtm[pc * 512:(pc + 1) * 512, :].rearrange("(b p) c -> p b c", p=128), ldb[s_], reads=[vtm_b], writes=[ldb[s_]])
                    V(lambda e: e.tensor_copy(out=vaug[:, pc * 4:(pc + 1) * 4, :, 0:64], in_=ld[s_][:].rearrange("p (b g d) -> p b g d", b=4, g=4)), [ldb[s_]], [vaugb])
                def jl_of(i):
                    return [jj for jj in range(3) if 0 <= i - 1 + jj < 32]

                def s1(i, g):
                    jl = jl_of(i)
                    for r in range(4):
                        h = 4 * g + r
                        po = (r % 2) * 64
                        qc = h // 2
                        S.mm([pb[r]], [(ps[:, r, jj * 128:(jj + 1) * 128], kd[po:po + 64, g, (i - 1 + jj) * 128:(i + jj) * 128], qn[po:po + 64, qc, i * 128:(i + 1) * 128], True, True) for jj in jl], [qcb[8 + g], qcb[qc]])

                def s2(i, g):
                    jl = jl_of(i)
                    lo, hi = jl[0] * 128, (jl[-1] + 1) * 128
                    for r in range(4):
                        h = 4 * g + r
                        b_ = (g % 2) * 4 + r
                        slope8 = 8.0 * (2.0 ** (-(h + 1) / 2.0))
                        V(lambda e: e.scalar_tensor_tensor(out=tmp[b_][:, lo:hi], in0=dm[:, lo:hi], scalar=slope8, in1=ps[:, r, lo:hi], op0=ALU.mult, op1=ALU.add), [dmb, pb[r]], [tmpb[b_]])
                        A(lambda e: e.activation(out=pt[b_][:, lo:hi], in_=tmp[b_][:, lo:hi], func=AF.Exp, scale=0.125), [tmpb[b_]], [ptb[b_]])

                def s3(i, g):
                    jl = jl_of(i)
                    ob = 4 + g % 2
                    y_ = i % 2
                    for r in range(4):
                        b_ = (g % 2) * 4 + r
                        S.mm([pb[ob]], [(ps[:, ob, r * 128:r * 128 + 65], pt[b_][:, jj * 128:(jj + 1) * 128], vaug[:, i - 1 + jj, g, 0:65], jj == jl[0], jj == jl[-1]) for jj in jl], [ptb[b_], vaugb])
                    d_ = g % 2
                    V(lambda e: e.tensor_tensor(out=den[:, d_ * 4:d_ * 4 + 4].rearrange("p (a b) -> p a b", b=1), in0=ps[:, ob, :].rearrange("p (a b) -> p a b", b=128)[:, :, 64:65], in1=esink[:, 4 * g:4 * g + 4].rearrange("p (a b) -> p a b", b=1), op=ALU.add), [pb[ob], esb], [denb[d_]])
                    V(lambda e: e.reciprocal(out=rec[:, d_ * 4:d_ * 4 + 4], in_=den[:, d_ * 4:d_ * 4 + 4]), [denb[d_]], [recb[d_]])
                    for r in range(4):
                        h = 4 * g + r
                        if g % 2 == 0:
                            A(lambda e: e.activation(out=yat[y_][:, h * 64:(h + 1) * 64], in_=ps[:, ob, r * 128:r * 128 + 64], func=AF.Copy, scale=rec[:, d_ * 4 + r:d_ * 4 + r + 1]), [pb[ob], recb[d_]], [yatb[y_][g]])
                        else:
                            V(lambda e: e.tensor_scalar(out=yat[y_][:, h * 64:(h + 1) * 64], in0=ps[:, ob, r * 128:r * 128 + 64], scalar1=rec[:, d_ * 4 + r:d_ * 4 + r + 1], scalar2=None, op0=ALU.mult), [pb[ob], recb[d_]], [yatb[y_][g]])

                def norm_ew(i):
                    y_ = i % 2
                    A(lambda e: e.activation(out=junk[:], in_=yat[y_][:], func=AF.Square, accum_out=ssq[:, y_:y_ + 1]), yatb[y_], [junkb, ssqb[y_]])
                    rsqrt(ssq[:, y_:y_ + 1], ssq[:, y_:y_ + 1], 1.0 / 1024, [ssqb[y_]], ssqb[y_])
                    V(lambda e: e.scalar_tensor_tensor(out=ybf[y_][:], in0=yat[y_][:], scalar=ssq[:, y_:y_ + 1], in1=gat[:], op0=ALU.mult, op1=ALU.mult), yatb[y_] + [ssqb[y_], gatb], [ybfb[y_]])

                def norm_tr(i):
                    y_ = i % 2
                    tb = 6 + i % 2
                    psT = ps[:, tb, :].bitcast(BF16)
                    S.tr([pb[tb]], [(psT[:, c * 128:(c + 1) * 128], ybf[y_][:, c * 128:(c + 1) * 128], identb[:]) for c in range(8)], [ybfb[y_], idb[0]])
                    t_ = (i // 4) % 2
                    A(lambda e: e.copy(out=yts[t_][:, :, (i % 4) * 128:(i % 4 + 1) * 128], in_=psT.rearrange("p (c t) -> p c t", c=8)), [pb[tb]], [ytsb[t_]])
                    if i % 4 == 3:
                        tsl = slice((i // 4) * 512, (i // 4 + 1) * 512)
                        S.dma(S.sp, ymix[1024:2048, tsl].rearrange("(c p) t -> p c t", p=128), yts[t_][:], ytsb[t_], reads=[ytsb[t_]], writes=ymix_b[8:16])

                s1(0, 0)
                for i in range(32):
                    for g in range(4):
                        s2(i, g)
                        if g < 3:
                            s1(i, g + 1)
                        elif i < 31:
                            s1(i + 1, 0)
                        s3(i, g)
                        if g == 1 and i > 0:
                            norm_tr(i - 1)
                    norm_ew(i)
                norm_tr(31)
                S.barrier()

        def mx_pass(l):
            mx_filter(l)
            mx_hyena(l)
            mx_attn(l)

        S.barrier()
        steps = [
            lambda: tl_pass(None, 0, xT, None, xres, xres_b),
            lambda: mx_pass(0),
            lambda: tl_pass(0, 1, xres, xres_b, xres, xres_b),
            lambda: mx_pass(1),
            lambda: tl_pass(1, None, xres, xres_b, yout, None),
        ]
        for i, fn in enumerate(steps):
            if stop_after is not None and i > stop_after:
                break
            fn()
        S.barrier()
    return nc


_CONST = {}


def _constants():
    if _CONST:
        return _CONST
    bf = ml_dtypes.bfloat16
    n = np.arange(L, dtype=np.float32)
    t = n / np.float32(L - 1)
    bands = np.linspace(1e-4, 15, 16, dtype=np.float32)
    ang = (np.float32(2.0 * math.pi / L) * n[:, None] * bands[None, :]).astype(np.float32)
    z = np.concatenate([t[:, None], np.cos(ang), np.sin(ang)], axis=-1).astype(np.float32)
    _CONST["zT"] = np.ascontiguousarray(z.T)
    min_decay = math.log(1e-2) / 1.5
    max_decay = math.log(1e-2) / 0.3
    deltas = np.abs(np.linspace(min_decay, max_decay, 512, dtype=np.float32))
    _CONST["winw"] = (np.exp(-t[:, None] * deltas[None, :]) + np.float32(0.05)).astype(np.float32)
    tt = np.arange(L, dtype=np.int64)
    ff = np.arange(L, dtype=np.int64)
    ph = ((2 * ff[None, :] + 1) * tt[:, None]) % 16384
    angm = ph.astype(np.float64) * (2.0 * math.pi / 16384.0)
    C = np.cos(angm).astype(np.float32)
    Sm = np.sin(angm).astype(np.float32)
    del angm, ph

    def fw(M):
        return np.ascontiguousarray(M.reshape(32, 128, 32, 128).transpose(2, 1, 0, 3).reshape(32, 128, 4096)).astype(bf)

    def inv(M):
        return np.ascontiguousarray(M.reshape(8, 512, 4, 8, 128).transpose(0, 2, 4, 3, 1).reshape(8, 4, 128, 4096)).astype(bf)

    _CONST["cfw"] = fw(C)
    _CONST["sfw"] = fw(Sm)
    _CONST["cinv"] = inv(C)
    _CONST["sinv"] = inv(-Sm)
    s = np.arange(128)[:, None]
    col = np.arange(384)[None, :]
    jj = col // 128
    q = col % 128
    dist = np.abs(128 * (jj - 1) + s - q)
    _CONST["dmat"] = np.where(dist <= 128, -dist.astype(np.float32), np.float32(-1.0e6)).astype(np.float32)
    _CONST["identb"] = np.eye(128, dtype=np.float32).astype(bf)
    _CONST["identf"] = np.eye(128, dtype=np.float32)
    return _CONST


def _tile_w(w, cols=None):
    K, N = w.shape
    return np.ascontiguousarray(w.reshape(K // 128, 128, N // 128, 128).transpose(2, 1, 0, 3).reshape(N // 128, 128, (K // 128) * 128))


def _col(v):
    return np.ascontiguousarray(np.asarray(v, np.float32).reshape(-1, 128).T)


def _shared_inputs(inp):
    m = dict(_constants())
    for l in range(DEPTH):
        for f, pre in ((1, "ffn1"), (2, "ffn2")):
            m["wg%d_%d" % (f, l)] = _tile_w(inp[pre + "_w_gate"][l])
            m["wu%d_%d" % (f, l)] = _tile_w(inp[pre + "_w_up"][l])
            m["wd%d_%d" % (f, l)] = np.ascontiguousarray(inp[pre + "_w_down"][l])
        win = inp["w_in"][l]
        kcols = win[:, 4096:4352]
        kdup = np.concatenate([np.concatenate([kcols[:, g * 64:(g + 1) * 64]] * 2, axis=1) for g in range(4)], axis=1)
        wfm = np.concatenate([win[:, :4096], kdup], axis=1)
        m["win_%d" % l] = _tile_w(wfm)
        wv = win[:, 4352:4608]
        m["winv_%d" % l] = np.ascontiguousarray(wv.reshape(16, 128, 256).transpose(1, 0, 2).reshape(128, 16 * 256))
        m["wout_%d" % l] = _tile_w(inp["w_out"][l])
        sm = np.zeros((128, NSM), np.float32)
        sm[:, C_GF1:C_GF1 + 16] = _col(inp["norm_ffn1"][l])
        sm[:, C_GMIX:C_GMIX + 16] = _col(inp["norm_mix"][l])
        sm[:, C_GF2:C_GF2 + 16] = _col(inp["norm_ffn2"][l])
        hsw = inp["hyena_short_w"][l]
        for tap in range(3):
            sm[:, C_HSW + tap:C_HSW + 36:3] = _col(hsw[tap])
        sm[:, C_HSB:C_HSB + 12] = _col(inp["hyena_short_b"][l])
        scw = inp["sconv_w"][l]
        for tap in range(3):
            sm[:, C_SCW + tap:C_SCW + 12:3] = _col(scw[tap])
        sm[:, C_GHS:C_GHS + 8] = _col(inp["mix_out_g"][l][:1024])
        sm[:, C_GQ] = np.concatenate([inp["q_norm_g"][l]] * 2)
        sm[:, C_GK] = np.concatenate([inp["k_norm_g"][l]] * 2)
        sm[:64, C_FREQ] = inp["filt_freq"][l]
        sm[:64, C_B1] = inp["filt_b1"][l]
        sm[:64, C_B2] = inp["filt_b2"][l]
        sm[:64, C_B3] = inp["filt_b3"][l]
        sm[:, C_SINK:C_SINK + 16] = np.broadcast_to(inp["attn_sink"][l][None, :], (128, 16))
        m["sm_%d" % l] = sm
        m["gat_%d" % l] = np.ascontiguousarray(np.broadcast_to(inp["mix_out_g"][l][None, 1024:], (128, 1024)))
        m["hbias_%d" % l] = np.ascontiguousarray(inp["hyena_bias"][l][None, :])
        m["fw1_%d" % l] = np.ascontiguousarray(inp["filt_w1"][l])
        m["fw2_%d" % l] = np.ascontiguousarray(inp["filt_w2"][l])
        m["fw3_%d" % l] = np.ascontiguousarray(inp["filt_w3"][l])
        m["fwo_%d" % l] = np.ascontiguousarray(inp["filt_w_out"][l])
    return m


def kernel(**inputs):
    inp = {k: np.asarray(v, dtype=np.float32) for k, v in inputs.items()}
    shared = _shared_inputs(inp)
    x = inp["x"]
    in_maps = []
    for b in range(NCORES):
        m = dict(shared)
        m["xT"] = np.ascontiguousarray(x[b].T)
        in_maps.append(m)
    nc = build_nc()
    res = run_bass_kernel_spmd(nc, in_maps, core_ids=list(range(NCORES)))
    out = np.stack([np.ascontiguousarray(res.results[b]["yout"].T) for b in range(NCORES)], axis=0)
    return out.astype(np.float32)
```

```python
import math
from contextlib import ExitStack

import numpy as np
import ml_dtypes

import concourse.bass as bass
import concourse.mybir as mybir
from concourse.bass_utils import run_bass_kernel_spmd

F32 = mybir.dt.float32
BF16 = mybir.dt.bfloat16
ALU = mybir.AluOpType
AF = mybir.ActivationFunctionType

D = 2048
L = 4096
FF = 5632
NFC = FF // 128
SEGC = 4
NSEG = NFC // SEGC
TT = 1024
NTT = L // TT
EPS = 1e-6
NCORES = 4
DEPTH = 2
NPROJ = 36

C_GF1, C_GMIX, C_GF2 = 0, 16, 32
C_HSW, C_HSB, C_SCW, C_GHS = 48, 84, 96, 108
C_GQ, C_GK, C_FREQ, C_B1, C_B2, C_B3, C_SINK = 116, 117, 118, 119, 120, 121, 122
NSM = 138


class Buf:
    __slots__ = ("w", "r", "dsem", "dcnt", "name", "kind")

    def __init__(self, name=""):
        self.w = None
        self.r = {}
        self.dsem = None
        self.dcnt = 0
        self.name = name
        self.kind = None


class Eng:
    def __init__(self, name, eng, sem, is_pe=False):
        self.name = name
        self.eng = eng
        self.sem = sem
        self.n = 0
        self.seen = {}
        self.is_pe = is_pe


class Sched:
    def __init__(self, nc, stack):
        self.nc = nc
        self.stack = stack
        mk = lambda nm: stack.enter_context(nc.semaphore(nm))
        self.pe = Eng("pe", nc.tensor, mk("s_pe"), True)
        self.dve = Eng("dve", nc.vector, mk("s_dve"))
        self.act = Eng("act", nc.scalar, mk("s_act"))
        self.pool = Eng("pool", nc.gpsimd, mk("s_pool"))
        self.sp = Eng("sp", nc.sync, mk("s_sp"))
        self.engs = [self.pe, self.dve, self.act, self.pool, self.sp]
        self.dmabufs = []
        self.sempool = {"sw": [], "hw": []}
        self.nsem = 0

    def _deps(self, reads, writes):
        d = {}
        for b in reads:
            if b.w is not None:
                o, v = b.w
                if d.get(o, 0) < v:
                    d[o] = v
        for b in writes:
            if b.w is not None:
                o, v = b.w
                if d.get(o, 0) < v:
                    d[o] = v
            for o, v in b.r.items():
                if d.get(o, 0) < v:
                    d[o] = v
        return d

    def _wait(self, e, d):
        for o, v in d.items():
            if o is e and e.is_pe:
                continue
            if e.seen.get(o, 0) >= v:
                continue
            sem = o.sem if isinstance(o, Eng) else o.dsem
            e.eng.wait_ge(sem, v)
            e.seen[o] = v

    def _record(self, tick, reads, writes):
        o, v = tick
        for b in reads:
            if b.r.get(o, 0) < v:
                b.r[o] = v
        for b in writes:
            b.w = tick
            b.r = {}

    def op(self, e, fn, reads=(), writes=()):
        self._wait(e, self._deps(reads, writes))
        ins = fn(e.eng)
        e.n += 1
        ins.then_inc(e.sem, 1)
        self._record((e, e.n), reads, writes)

    def mm(self, outs, mms, reads, per=None):
        e = self.pe
        self._wait(e, self._deps(reads, outs))
        ins = None
        allr = list(reads)
        for i, (o, l, r, st, sp) in enumerate(mms):
            if per is not None:
                self._wait(e, self._deps(per[i], ()))
                allr.extend(per[i])
            ins = e.eng.matmul(o, lhsT=l, rhs=r, start=st, stop=sp)
        e.n += 1
        ins.then_inc(e.sem, 1)
        self._record((e, e.n), allr, outs)

    def tr(self, outs, trs, reads):
        e = self.pe
        self._wait(e, self._deps(reads, outs))
        ins = None
        for (o, i, idn) in trs:
            ins = e.eng.transpose(o, i, idn)
        e.n += 1
        ins.then_inc(e.sem, 1)
        self._record((e, e.n), reads, outs)

    def dma(self, q, out, in_, sb, reads=(), writes=()):
        kind = "sw" if q is self.pool else "hw"
        if sb.dsem is None:
            pool = self.sempool[kind]
            if pool:
                sb.dsem, sb.dcnt = pool.pop()
            else:
                sb.dsem = self.stack.enter_context(self.nc.semaphore("d%s%d" % (kind, self.nsem)))
                self.nsem += 1
            sb.kind = kind
            self.dmabufs.append(sb)
        assert sb.kind == kind, "a buffer must be DMA'd through one kind of queue only"
        d = self._deps(reads, writes)
        if sb.dcnt and d.get(sb, 0) < 16 * sb.dcnt:
            d[sb] = 16 * sb.dcnt
        self._wait(q, d)
        q.eng.dma_start(out=out, in_=in_).then_inc(sb.dsem, 16)
        sb.dcnt += 1
        self._record((sb, 16 * sb.dcnt), reads, writes)

    def barrier(self):
        for e in self.engs:
            for o in self.engs:
                if o is e or o.n == 0:
                    continue
                if e.seen.get(o, 0) < o.n:
                    e.eng.wait_ge(o.sem, o.n)
                    e.seen[o] = o.n
            for b in self.dmabufs:
                if b.dcnt and e.seen.get(b, 0) < 16 * b.dcnt:
                    e.eng.wait_ge(b.dsem, 16 * b.dcnt)
                    e.seen[b] = 16 * b.dcnt
        for b in self.dmabufs:
            for e in self.engs:
                e.seen[b] = 16 * b.dcnt
            self.sempool[b.kind].append((b.dsem, b.dcnt))
            b.dsem = None
        self.dmabufs = []


def build_nc(stop_after=None, dbg=False):
    nc = bass.Bass("TRN2", target_bir_lowering=False)
    dt_in = lambda n, s, d=F32: nc.dram_tensor(n, list(s), d, kind="ExternalInput").ap()
    okind = "ExternalOutput" if dbg else "Internal"
    dt_sc = lambda n, s, d=F32: nc.dram_tensor(n, list(s), d, kind=okind).ap()

    xT = dt_in("xT", [D, L])
    W = []
    for l in range(DEPTH):
        w = {}
        for f in (1, 2):
            w["wg%d" % f] = dt_in("wg%d_%d" % (f, l), [NFC, 128, D])
            w["wu%d" % f] = dt_in("wu%d_%d" % (f, l), [NFC, 128, D])
            w["wd%d" % f] = dt_in("wd%d_%d" % (f, l), [FF, D])
        w["win"] = dt_in("win_%d" % l, [NPROJ, 128, D])
        w["winv"] = dt_in("winv_%d" % l, [128, 16 * 256])
        w["wout"] = dt_in("wout_%d" % l, [16, 128, D])
        w["sm"] = dt_in("sm_%d" % l, [128, NSM])
        w["gat"] = dt_in("gat_%d" % l, [128, 1024])
        w["hbias"] = dt_in("hbias_%d" % l, [1, 512])
        w["fw1"] = dt_in("fw1_%d" % l, [33, 64])
        w["fw2"] = dt_in("fw2_%d" % l, [64, 64])
        w["fw3"] = dt_in("fw3_%d" % l, [64, 64])
        w["fwo"] = dt_in("fwo_%d" % l, [64, 1024])
        W.append(w)
    zT_d = dt_in("zT", [33, L])
    winw_d = dt_in("winw", [L, 512])
    cfw_d = dt_in("cfw", [32, 128, 4096], BF16)
    sfw_d = dt_in("sfw", [32, 128, 4096], BF16)
    cinv_d = dt_in("cinv", [8, 4, 128, 4096], BF16)
    sinv_d = dt_in("sinv", [8, 4, 128, 4096], BF16)
    dmat_d = dt_in("dmat", [128, 384])
    identb_d = dt_in("identb", [128, 128], BF16)
    identf_d = dt_in("identf", [128, 128])

    yout = nc.dram_tensor("yout", [D, L], F32, kind="ExternalOutput").ap()
    xres = dt_sc("xres", [D, L])
    proj = dt_sc("proj", [NPROJ * 128, L])
    vtm = dt_sc("vtm", [L, 256])
    ymix = dt_sc("ymix", [D, L], BF16)
    ux0 = dt_sc("ux0", [512, L])
    kspec = dt_sc("kspec", [2, L, 512])
    qkn = dt_sc("qkn", [12 * 128, L], BF16)

    xres_b = [[Buf() for _ in range(16)] for _ in range(NTT)]
    proj_b = [Buf() for _ in range(NPROJ)]
    vtm_b = Buf()
    ymix_b = [Buf() for _ in range(16)]
    ux0_b = [Buf() for _ in range(4)]
    kspec_b = [Buf() for _ in range(32)]
    qkn_b = [Buf() for _ in range(12)]

    with ExitStack() as gs:
        S = Sched(nc, gs)
        V = lambda fn, r, w: S.op(S.dve, fn, r, w)
        A = lambda fn, r, w: S.op(S.act, fn, r, w)
        uniq = [0]

        def sb(st, n, s, d=F32):
            uniq[0] += 1
            return st.enter_context(nc.sbuf_tensor("%s_%d" % (n, uniq[0]), list(s), d))

        ps = gs.enter_context(nc.psum_tensor("ps", [128, 8, 512], F32))
        pb = [Buf("pb%d" % i) for i in range(8)]

        ones = sb(gs, "ones", [128, 128], BF16)
        blk64 = sb(gs, "blk64", [128, 128], BF16)
        identb = sb(gs, "identb_s", [128, 128], BF16)
        identf = sb(gs, "identf_s", [128, 128])
        negpi = sb(gs, "negpi", [128, 1])
        epst = sb(gs, "epst", [128, 1])
        zerot = sb(gs, "zerot", [128, 1])
        smt = [sb(gs, "smt%d" % l, [128, NSM]) for l in range(DEPTH)]
        cb = Buf("const")
        smb = [Buf(), Buf()]
        idb = [Buf(), Buf()]
        V(lambda e: e.memset(ones[:], 1.0), [], [cb])
        V(lambda e: e.memset(blk64[:], 0.0), [], [cb])
        V(lambda e: e.memset(blk64[0:64, 0:64], 1.0), [], [cb])
        V(lambda e: e.memset(blk64[64:128, 64:128], 1.0), [], [cb])
        V(lambda e: e.memset(negpi[:], -math.pi), [], [cb])
        V(lambda e: e.memset(epst[:], EPS), [], [cb])
        V(lambda e: e.memset(zerot[:], 0.0), [], [cb])

        def rsqrt(out, in_, scale, reads, wbuf):
            A(lambda e: e.activation(out=out, in_=in_, func=AF.Sqrt, bias=epst[:, 0:1], scale=scale), list(reads) + [cb], [wbuf])
            V(lambda e: e.reciprocal(out=out, in_=out), [wbuf], [wbuf])
        for l in range(DEPTH):
            S.dma(S.sp, smt[l][:], W[l]["sm"], smb[l], writes=[smb[l]])
        S.dma(S.sp, identb[:], identb_d, idb[0], writes=[idb[0]])
        S.dma(S.sp, identf[:], identf_d, idb[1], writes=[idb[1]])

        def tl_pass(l_prev, l_next, src, src_b, dst, dst_b):
            with ExitStack() as st:
                xt = sb(st, "xt", [128, 16, TT])
                hT = sb(st, "hT", [128, 16, TT], BF16)
                act = sb(st, "act", [128, SEGC, TT], BF16)
                wgt = [sb(st, "wgt%d" % i, [128, 16, 128], BF16) for i in range(2)]
                wut = [sb(st, "wut%d" % i, [128, 16, 128], BF16) for i in range(2)]
                wdt = [sb(st, "wdt%d" % i, [128, SEGC, 512], BF16) for i in range(2)]
                wvt = sb(st, "wvt", [128, 16, 256], BF16)
                silu = [sb(st, "silu%d" % i, [128, 512]) for i in range(2)]
                rstd = sb(st, "rstd", [128, TT])
                sq = [sb(st, "sq%d" % i, [128, TT], BF16) for i in range(2)]
                stg = [sb(st, "stg%d" % i, [128, TT]) for i in range(2)]
                vstg = sb(st, "vstg", [128, 8, 256])
                qstg = [sb(st, "qstg%d" % i, [128, TT], BF16) for i in range(2)]
                qsq = [sb(st, "qsq%d" % i, [128, 512], BF16) for i in range(2)]
                qrs = [sb(st, "qrs%d" % i, [128, 512]) for i in range(2)]
                qstgb = [Buf(), Buf()]
                qsqb = [Buf(), Buf()]
                qrsb = [Buf(), Buf()]
                xb = [Buf() for _ in range(16)]
                hb = [[Buf(), Buf()] for _ in range(16)]
                actb = [Buf() for _ in range(SEGC)]
                wgb = [Buf(), Buf()]
                wub = [Buf(), Buf()]
                wdb = [Buf(), Buf()]
                wvb = Buf()
                silub = [Buf(), Buf()]
                rstdb = [Buf(), Buf()]
                sqb = [Buf(), Buf()]
                stgb = [Buf(), Buf()]
                vstgb = Buf()
                cnt = {"w": 0, "wd": 0, "si": 0, "db": 0, "stg": 0, "bk": 0, "qs": 0, "qb": 0}

                def nxt(k, m=2):
                    v = cnt[k]
                    cnt[k] = (v + 1) % m
                    return v

                def norm(gcol):
                    for kc in range(16):
                        s_ = kc % 2
                        A(lambda e: e.activation(out=sq[s_][:], in_=xt[:, kc, :], func=AF.Square), [xb[kc]], [sqb[s_]])
                        for hf in range(2):
                            S.mm([pb[6 + hf]], [(ps[:, 6 + hf, :], ones[:], sq[s_][:, hf * 512:(hf + 1) * 512], kc == 0, kc == 15)], [sqb[s_], cb])
                    for hf in range(2):
                        sl = slice(hf * 512, (hf + 1) * 512)
                        rsqrt(rstd[:, sl], ps[:, 6 + hf, :], 1.0 / D, [pb[6 + hf]], rstdb[hf])
                        for kc in range(16):
                            V(lambda e: e.scalar_tensor_tensor(out=hT[:, kc, sl], in0=xt[:, kc, sl], scalar=gcol[:, kc:kc + 1], in1=rstd[:, sl], op0=ALU.mult, op1=ALU.mult), [xb[kc], rstdb[hf]], [hb[kc][hf]])

                def hper(hf):
                    return [[hb[kc][hf]] for kc in range(16)]

                def ffn(wg, wu, wd):
                    for seg in range(NSEG):
                        for cl in range(SEGC):
                            c = seg * SEGC + cl
                            s_ = nxt("w")
                            S.dma(S.pool, wgt[s_][:], wg[c].rearrange("p (k j) -> p k j", k=16), wgb[s_], writes=[wgb[s_]])
                            S.dma(S.pool, wut[s_][:], wu[c].rearrange("p (k j) -> p k j", k=16), wub[s_], writes=[wub[s_]])
                            for hf in range(2):
                                sl = slice(hf * 512, (hf + 1) * 512)
                                S.mm([pb[hf * 2]], [(ps[:, hf * 2, :], wgt[s_][:, kc, :], hT[:, kc, sl], kc == 0, kc == 15) for kc in range(16)], [wgb[s_]], per=hper(hf))
                                S.mm([pb[hf * 2 + 1]], [(ps[:, hf * 2 + 1, :], wut[s_][:, kc, :], hT[:, kc, sl], kc == 0, kc == 15) for kc in range(16)], [wub[s_]], per=hper(hf))
                                si = nxt("si")
                                A(lambda e: e.activation(out=silu[si][:], in_=ps[:, hf * 2, :], func=AF.Silu), [pb[hf * 2]], [silub[si]])
                                V(lambda e: e.tensor_tensor(out=act[:, cl, sl], in0=silu[si][:], in1=ps[:, hf * 2 + 1, :], op=ALU.mult), [silub[si], pb[hf * 2 + 1]], [actb[cl]])
                        for dq in range(4):
                            s_ = nxt("wd")
                            S.dma(S.pool, wdt[s_][:], wd[seg * 512:(seg + 1) * 512, dq * 512:(dq + 1) * 512].rearrange("(c p) n -> p c n", p=128), wdb[s_], writes=[wdb[s_]])
                            for dd in range(4):
                                d = dq * 4 + dd
                                for hf in range(2):
                                    sl = slice(hf * 512, (hf + 1) * 512)
                                    bk = 4 + nxt("db", 4)
                                    S.mm([pb[bk]], [(ps[:, bk, :], wdt[s_][:, cl, dd * 128:(dd + 1) * 128], act[:, cl, sl], cl == 0, cl == SEGC - 1) for cl in range(SEGC)], [wdb[s_]] + actb)
                                    V(lambda e: e.scalar_tensor_tensor(out=xt[:, d, sl], in0=ps[:, bk, :], scalar=0.5, in1=xt[:, d, sl], op0=ALU.mult, op1=ALU.add), [pb[bk], xb[d]], [xb[d]])

                def inproj(w, tt, w_sm, w_smb):
                    tsl = slice(tt * TT, (tt + 1) * TT)
                    for j in range(NPROJ):
                        s_ = nxt("w")
                        S.dma(S.pool, wgt[s_][:], w["win"][j].rearrange("p (k j) -> p k j", k=16), wgb[s_], writes=[wgb[s_]])
                        g_ = nxt("stg")
                        for hf in range(2):
                            sl = slice(hf * 512, (hf + 1) * 512)
                            bk = nxt("bk", 4)
                            S.mm([pb[bk]], [(ps[:, bk, :], wgt[s_][:, kc, :], hT[:, kc, sl], kc == 0, kc == 15) for kc in range(16)], [wgb[s_]], per=hper(hf))
                            if j < 24:
                                A(lambda e: e.copy(out=stg[g_][:, sl], in_=ps[:, bk, :]), [pb[bk]], [stgb[g_]])
                            else:
                                q_ = nxt("qs")
                                nb = 4 + nxt("qb", 4)
                                gcol = w_sm[:, C_GQ:C_GQ + 1] if j < 32 else w_sm[:, C_GK:C_GK + 1]
                                A(lambda e: e.activation(out=qsq[q_][:], in_=ps[:, bk, :], func=AF.Square), [pb[bk]], [qsqb[q_]])
                                S.mm([pb[nb]], [(ps[:, nb, :], blk64[:], qsq[q_][:], True, True)], [qsqb[q_], cb])
                                A(lambda e: e.activation(out=qrs[q_][:], in_=ps[:, nb, :], func=AF.Ln, bias=epst[:, 0:1], scale=1.0 / 64), [pb[nb], cb], [qrsb[q_]])
                                A(lambda e: e.activation(out=qrs[q_][:], in_=qrs[q_][:], func=AF.Exp, scale=-0.5), [qrsb[q_]], [qrsb[q_]])
                                V(lambda e: e.scalar_tensor_tensor(out=qstg[g_][:, sl], in0=ps[:, bk, :], scalar=gcol, in1=qrs[q_][:], op0=ALU.mult, op1=ALU.mult), [pb[bk], qrsb[q_], w_smb], [qstgb[g_]])
                        if j < 24:
                            S.dma(S.act, proj[j * 128:(j + 1) * 128, tsl], stg[g_][:], stgb[g_], reads=[stgb[g_]], writes=[proj_b[j]])
                        else:
                            S.dma(S.act, qkn[(j - 24) * 128:(j - 23) * 128, tsl], qstg[g_][:], qstgb[g_], reads=[qstgb[g_]], writes=[qkn_b[j - 24]])
                    S.dma(S.pool, wvt[:], w["winv"].rearrange("p (k j) -> p k j", k=16), wvb, writes=[wvb])
                    for blk in range(8):
                        bk = nxt("bk", 4)
                        S.mm([pb[bk]], [(ps[:, bk, 0:256], hT[:, kc, blk * 128:(blk + 1) * 128], wvt[:, kc, :], kc == 0, kc == 15) for kc in range(16)], [wvb], per=hper(blk // 4))
                        A(lambda e: e.copy(out=vstg[:, blk, :], in_=ps[:, bk, 0:256]), [pb[bk]], [vstgb])
                    S.dma(S.act, vtm[tsl, :].rearrange("(b p) c -> p b c", p=128), vstg[:], vstgb, reads=[vstgb], writes=[vtm_b])

                def outproj(w, tt):
                    tsl = slice(tt * TT, (tt + 1) * TT)
                    for kc in range(16):
                        S.dma(S.sp, hT[:, kc, :], ymix[kc * 128:(kc + 1) * 128, tsl], hb[kc][0], reads=[ymix_b[kc]], writes=[hb[kc][0], hb[kc][1]])
                    for d in range(16):
                        s_ = nxt("w")
                        S.dma(S.pool, wgt[s_][:], w["wout"][d].rearrange("p (k j) -> p k j", k=16), wgb[s_], writes=[wgb[s_]])
                        for hf in range(2):
                            sl = slice(hf * 512, (hf + 1) * 512)
                            bk = 4 + nxt("db", 4)
                            S.mm([pb[bk]], [(ps[:, bk, :], wgt[s_][:, kc, :], hT[:, kc, sl], kc == 0, kc == 15) for kc in range(16)], [wgb[s_]], per=hper(hf))
                            V(lambda e: e.tensor_tensor(out=xt[:, d, sl], in0=ps[:, bk, :], in1=xt[:, d, sl], op=ALU.add), [pb[bk], xb[d]], [xb[d]])

                def xload(tt):
                    tsl = slice(tt * TT, (tt + 1) * TT)
                    for d in range(16):
                        S.dma(S.sp, xt[:, d, :], src[d * 128:(d + 1) * 128, tsl], xb[d], reads=([src_b[tt][d]] if src_b else []), writes=[xb[d]])

                def xstore(tt):
                    tsl = slice(tt * TT, (tt + 1) * TT)
                    for d in range(16):
                        S.dma(S.sp, dst[d * 128:(d + 1) * 128, tsl], xt[:, d, :], xb[d], reads=[xb[d]], writes=([dst_b[tt][d]] if dst_b else []))

                xload(0)
                for tt in range(NTT):
                    if l_prev is not None:
                        w = W[l_prev]
                        outproj(w, tt)
                        norm(smt[l_prev][:, C_GF2:C_GF2 + 16])
                        ffn(w["wg2"], w["wu2"], w["wd2"])
                    if l_next is not None:
                        w = W[l_next]
                        norm(smt[l_next][:, C_GF1:C_GF1 + 16])
                        ffn(w["wg1"], w["wu1"], w["wd1"])
                        norm(smt[l_next][:, C_GMIX:C_GMIX + 16])
                    xstore(tt)
                    if tt + 1 < NTT:
                        xload(tt + 1)
                    if l_next is not None:
                        inproj(W[l_next], tt, smt[l_next], smb[l_next])
                S.barrier()

        def conv3(pad, u, padb, ub, w0, w1, w2, bias, n=L):
            P = lambda fn, r, w: S.op(S.pool, fn, r, w)
            A(lambda e: e.activation(out=u[:, 0:n], in_=pad[:, 1:n + 1], func=AF.Identity, bias=(bias if bias is not None else zerot[:, 0:1]), scale=w1), [padb, cb], [ub])
            V(lambda e: e.scalar_tensor_tensor(out=u[:, 0:n], in0=pad[:, 0:n], scalar=w0, in1=u[:, 0:n], op0=ALU.mult, op1=ALU.add), [padb, ub], [ub])
            V(lambda e: e.scalar_tensor_tensor(out=u[:, 0:n], in0=pad[:, 2:n + 2], scalar=w2, in1=u[:, 0:n], op0=ALU.mult, op1=ALU.add), [padb, ub], [ub])

        def mx_filter(l):
            w = W[l]
            sm = smt[l]
            with ExitStack() as sk:
              kp = sb(sk, "kp", [128, 32, 512], BF16)
              km = sb(sk, "km", [128, 32, 512], BF16)
              kpb, kmb = Buf(), Buf()
              with ExitStack() as st:
                zt = sb(st, "zt", [64, L])
                hA = sb(st, "hA", [64, L])
                hB = sb(st, "hB", [64, L])
                ut = [sb(st, "ut%d" % i, [64, 512]) for i in range(2)]
                ut2 = [sb(st, "utb%d" % i, [64, 512]) for i in range(2)]
                ut2b = [Buf(), Buf()]
                w1t = sb(st, "w1t", [64, 64])
                w2t = sb(st, "w2t", [64, 64])
                w3t = sb(st, "w3t", [64, 64])
                wot = sb(st, "wot", [64, 1024])
                s1 = sb(st, "s1", [64, 4])
                s2 = sb(st, "s2", [64, 4])
                hbt = sb(st, "hbt", [1, 512])
                hbs = [sb(st, "hbs%d" % i, [128, 512]) for i in range(2)]
                t1 = [sb(st, "t1%d" % i, [128, 512]) for i in range(2)]
                t2 = [sb(st, "t2%d" % i, [128, 512]) for i in range(2)]
                wint = [sb(st, "wint%d" % i, [128, 512]) for i in range(2)]
                ztb, hAb, hBb = Buf(), Buf(), Buf()
                utb = [Buf(), Buf()]
                wb = [Buf() for _ in range(5)]
                sb12 = Buf()
                hbsb = [Buf(), Buf()]
                t1b = [Buf(), Buf()]
                t2b = [Buf(), Buf()]
                wintb = [Buf(), Buf()]
                S.dma(S.sp, zt[0:33, :], zT_d, ztb, writes=[ztb])
                S.dma(S.sp, w1t[0:33, :], w["fw1"], wb[0], writes=[wb[0]])
                S.dma(S.sp, w2t[:], w["fw2"], wb[1], writes=[wb[1]])
                S.dma(S.sp, w3t[:], w["fw3"], wb[2], writes=[wb[2]])
                S.dma(S.sp, wot[:], w["fwo"], wb[3], writes=[wb[3]])
                S.dma(S.sp, hbt[:], w["hbias"], wb[4], writes=[wb[4]])
                V(lambda e: e.tensor_scalar(out=s1[:, 0:1], in0=sm[0:64, C_FREQ:C_FREQ + 1], scalar1=1.0 / (2 * math.pi), scalar2=None, op0=ALU.mult), [smb[l]], [sb12])
                for i in range(3):
                    V(lambda e: e.tensor_scalar(out=s2[:, i:i + 1], in0=sm[0:64, C_B1 + i:C_B1 + i + 1], scalar1=s1[:, 0:1], scalar2=None, op0=ALU.mult), [smb[l], sb12], [sb12])
                layers = [(w1t, 33, zt, ztb, hA, hAb, wb[0]), (w2t, 64, hA, hAb, hB, hBb, wb[1]), (w3t, 64, hB, hBb, hA, hAb, wb[2])]
                bkc = [0]
                for i, (wt, kd, xin, xinb, hout, houtb, wtb) in enumerate(layers):
                    for n in range(8):
                        sl = slice(n * 512, (n + 1) * 512)
                        bk = bkc[0]
                        bkc[0] = (bk + 1) % 4
                        S.mm([pb[bk]], [(ps[0:64, bk, :], wt[0:kd, :], xin[0:kd, sl], True, True)], [wtb, xinb])
                        u_ = n % 2
                        V(lambda e: e.tensor_scalar(out=ut[u_][:], in0=ps[0:64, bk, :], scalar1=s1[:, 0:1], scalar2=s2[:, i:i + 1], op0=ALU.mult, op1=ALU.add), [pb[bk], sb12], [utb[u_]])
                        V(lambda e: e.tensor_scalar(out=ut2[u_][:], in0=ut[u_][:], scalar1=12582912.0, scalar2=None, op0=ALU.add), [utb[u_]], [ut2b[u_]])
                        V(lambda e: e.tensor_scalar(out=ut2[u_][:], in0=ut2[u_][:], scalar1=-12582912.0, scalar2=None, op0=ALU.add), [ut2b[u_]], [ut2b[u_]])
                        V(lambda e: e.tensor_tensor(out=ut[u_][:], in0=ut[u_][:], in1=ut2[u_][:], op=ALU.subtract), [utb[u_], ut2b[u_]], [utb[u_]])
                        A(lambda e: e.activation(out=hout[:, sl], in_=ut[u_][:], func=AF.Sin, scale=2 * math.pi), [utb[u_], cb], [houtb])
                h3, h3b = hA, hAb
                for nb in range(32):
                    i_ = nb % 2
                    b0, b1 = (0, 1) if i_ == 0 else (2, 3)
                    S.mm([pb[b0]], [(ps[:, b0, :], h3[:, nb * 128:(nb + 1) * 128], wot[:, 0:512], True, True)], [h3b, wb[3]])
                    S.mm([pb[b1]], [(ps[:, b1, :], h3[:, nb * 128:(nb + 1) * 128], wot[:, 512:1024], True, True)], [h3b, wb[3]])
                    S.dma(S.sp, wint[i_][:], winw_d[nb * 128:(nb + 1) * 128, :], wintb[i_], writes=[wintb[i_]])
                    A(lambda e: e.copy(out=hbs[i_][:], in_=ps[:, b1, :]), [pb[b1]], [hbsb[i_]])
                    if nb == 0:
                        V(lambda e: e.memset(hbs[i_][0:1, :], 0.0), [], [hbsb[i_]])
                    V(lambda e: e.tensor_tensor(out=t1[i_][:], in0=ps[:, b0, :], in1=hbs[i_][:], op=ALU.add), [pb[b0], hbsb[i_]], [t1b[i_]])
                    V(lambda e: e.tensor_tensor(out=t2[i_][:], in0=hbs[i_][:], in1=ps[:, b0, :], op=ALU.subtract), [pb[b0], hbsb[i_]], [t2b[i_]])
                    if nb == 0:
                        V(lambda e: e.tensor_tensor(out=t1[i_][:], in0=t1[i_][:], in1=wint[i_][:], op=ALU.mult), [t1b[i_], wintb[i_]], [t1b[i_]])
                        V(lambda e: e.tensor_tensor(out=t2[i_][:], in0=t2[i_][:], in1=wint[i_][:], op=ALU.mult), [t2b[i_], wintb[i_]], [t2b[i_]])
                        V(lambda e: e.tensor_tensor(out=t1[i_][0:1, :], in0=t1[i_][0:1, :], in1=hbt[0:1, :], op=ALU.add), [t1b[i_], wb[4]], [t1b[i_]])
                        V(lambda e: e.tensor_tensor(out=t2[i_][0:1, :], in0=t2[i_][0:1, :], in1=hbt[0:1, :], op=ALU.subtract), [t2b[i_], wb[4]], [t2b[i_]])
                        V(lambda e: e.tensor_copy(out=kp[:, nb, :], in_=t1[i_][:]), [t1b[i_]], [kpb])
                        V(lambda e: e.tensor_copy(out=km[:, nb, :], in_=t2[i_][:]), [t2b[i_]], [kmb])
                    else:
                        V(lambda e: e.tensor_tensor(out=kp[:, nb, :], in0=t1[i_][:], in1=wint[i_][:], op=ALU.mult), [t1b[i_], wintb[i_]], [kpb])
                        V(lambda e: e.tensor_tensor(out=km[:, nb, :], in0=t2[i_][:], in1=wint[i_][:], op=ALU.mult), [t2b[i_], wintb[i_]], [kmb])
                S.barrier()
              with ExitStack() as st:
                ct = [sb(st, "fct%d" % i, [128, 16, 128], BF16) for i in range(4)]
                stl = [sb(st, "fst%d" % i, [128, 16, 128], BF16) for i in range(4)]
                kstg = [sb(st, "kstg%d" % i, [128, 2, 512]) for i in range(2)]
                ctb = [Buf() for _ in range(4)]
                stb = [Buf() for _ in range(4)]
                kstgb = [Buf(), Buf()]
                gen = sconv_units(l, st)
                for fch in range(32):
                    s_ = fch % 2
                    sl_ = [(2 * fch + hh) % 4 for hh in range(2)]
                    for hh in range(2):
                        S.dma(S.sp, ct[sl_[hh]][:], cfw_d[fch][:, hh * 2048:(hh + 1) * 2048].rearrange("p (k j) -> p k j", k=16), ctb[sl_[hh]], writes=[ctb[sl_[hh]]])
                        S.dma(S.sp, stl[sl_[hh]][:], sfw_d[fch][:, hh * 2048:(hh + 1) * 2048].rearrange("p (k j) -> p k j", k=16), stb[sl_[hh]], writes=[stb[sl_[hh]]])
                    b0, b1 = (4, 5) if s_ == 0 else (6, 7)
                    S.mm([pb[b0]], [(ps[:, b0, :], ct[sl_[tc // 16]][:, tc % 16, :], kp[:, tc, :], tc == 0, tc == 31) for tc in range(32)], [kpb], per=[[ctb[sl_[tc // 16]]] for tc in range(32)])
                    S.mm([pb[b1]], [(ps[:, b1, :], stl[sl_[tc // 16]][:, tc % 16, :], km[:, tc, :], tc == 0, tc == 31) for tc in range(32)], [kmb], per=[[stb[sl_[tc // 16]]] for tc in range(32)])
                    A(lambda e: e.activation(out=kstg[s_][:, 0, :], in_=ps[:, b0, :], func=AF.Copy, scale=2.0 / 8192.0), [pb[b0]], [kstgb[s_]])
                    A(lambda e: e.activation(out=kstg[s_][:, 1, :], in_=ps[:, b1, :], func=AF.Copy, scale=2.0 / 8192.0), [pb[b1]], [kstgb[s_]])
                    S.dma(S.act, kspec[:, fch * 128:(fch + 1) * 128, :].rearrange("a p c -> p a c"), kstg[s_][:], kstgb[s_], reads=[kstgb[s_]], writes=[kspec_b[fch]])
                    if fch % 2 == 1:
                        next(gen, None)
                for _ in gen:
                    pass
                S.barrier()

        def mx_hyena(l):
            sm = smt[l]
            with ExitStack() as sz:
                zvT = sb(sz, "zvT", [128, 32, 512], BF16)
                zvTb = Buf()
                with ExitStack() as st:
                    pads = [sb(st, "hpad%d" % i, [128, L + 2]) for i in range(4)]
                    us = [sb(st, "hu%d" % i, [128, L]) for i in range(3)]
                    padb = [Buf() for _ in range(4)]
                    ub = [Buf() for _ in range(3)]
                    for i in range(4):
                        V(lambda e: e.memset(pads[i][:, 0:1], 0.0), [], [padb[i]])
                        V(lambda e: e.memset(pads[i][:, L + 1:L + 2], 0.0), [], [padb[i]])

                    def taps(j):
                        c0 = C_HSW + j * 3
                        return sm[:, c0:c0 + 1], sm[:, c0 + 1:c0 + 2], sm[:, c0 + 2:c0 + 3], sm[:, C_HSB + j:C_HSB + j + 1]

                    k_ = 0
                    for j in range(4):
                        pis = []
                        for part in range(3):
                            pi = k_ % 4
                            k_ += 1
                            pis.append(pi)
                            pj = part * 4 + j
                            S.dma(S.sp, pads[pi][:, 1:L + 1], proj[pj * 128:(pj + 1) * 128, :], padb[pi], reads=[proj_b[pj]], writes=[padb[pi]])
                        for part in range(3):
                            pi = pis[part]
                            pj = part * 4 + j
                            w0, w1, w2, bb = taps(pj)
                            conv3(pads[pi], us[part], padb[pi], ub[part], w0, w1, w2, bb)
                        S.dma(S.act, ux0[j * 128:(j + 1) * 128, :], us[0][:], ub[0], reads=[ub[0]], writes=[ux0_b[j]])
                        uA, uB, uAb, uBb = us[1], us[2], ub[1], ub[2]
                        V(lambda e: e.tensor_tensor(out=uB[:], in0=uB[:], in1=uA[:], op=ALU.mult), [uAb, uBb], [uBb])
                        for b4 in range(8):
                            bk = b4 % 4
                            S.tr([pb[bk]], [(ps[:, bk, q * 128:(q + 1) * 128], uB[:, (b4 * 4 + q) * 128:(b4 * 4 + q + 1) * 128], identf[:]) for q in range(4)], [uBb, idb[1]])
                            A(lambda e: e.copy(out=zvT[:, b4 * 4:(b4 + 1) * 4, j * 128:(j + 1) * 128], in_=ps[:, bk, :].rearrange("p (q c) -> p q c", q=4)), [pb[bk]], [zvTb])
                    S.barrier()
                with ExitStack() as sy:
                    Yre = sb(sy, "Yre", [128, 32, 512], BF16)
                    Yim = sb(sy, "Yim", [128, 32, 512], BF16)
                    Yreb, Yimb = Buf(), Buf()
                    with ExitStack() as st:
                        ct = [sb(st, "zct%d" % i, [128, 16, 128], BF16) for i in range(4)]
                        stl = [sb(st, "zst%d" % i, [128, 16, 128], BF16) for i in range(4)]
                        kld = [sb(st, "kld%d" % i, [128, 2, 512]) for i in range(2)]
                        ta = [sb(st, "ta%d" % i, [128, 512]) for i in range(4)]
                        ctb = [Buf() for _ in range(4)]
                        stb = [Buf() for _ in range(4)]
                        kldb = [Buf(), Buf()]
                        tab = [Buf() for _ in range(4)]
                        for fch in range(32):
                            s_ = fch % 2
                            sl_ = [(2 * fch + hh) % 4 for hh in range(2)]
                            for hh in range(2):
                                S.dma(S.sp, ct[sl_[hh]][:], cfw_d[fch][:, hh * 2048:(hh + 1) * 2048].rearrange("p (k j) -> p k j", k=16), ctb[sl_[hh]], writes=[ctb[sl_[hh]]])
                                S.dma(S.sp, stl[sl_[hh]][:], sfw_d[fch][:, hh * 2048:(hh + 1) * 2048].rearrange("p (k j) -> p k j", k=16), stb[sl_[hh]], writes=[stb[sl_[hh]]])
                            S.dma(S.act, kld[s_][:], kspec[:, fch * 128:(fch + 1) * 128, :].rearrange("a p c -> p a c"), kldb[s_], reads=[kspec_b[fch]], writes=[kldb[s_]])
                            b0, b1 = (0, 1) if s_ == 0 else (2, 3)
                            S.mm([pb[b0]], [(ps[:, b0, :], ct[sl_[tc // 16]][:, tc % 16, :], zvT[:, tc, :], tc == 0, tc == 31) for tc in range(32)], [zvTb], per=[[ctb[sl_[tc // 16]]] for tc in range(32)])
                            S.mm([pb[b1]], [(ps[:, b1, :], stl[sl_[tc // 16]][:, tc % 16, :], zvT[:, tc, :], tc == 0, tc == 31) for tc in range(32)], [zvTb], per=[[stb[sl_[tc // 16]]] for tc in range(32)])
                            kre, kim = kld[s_][:, 0, :], kld[s_][:, 1, :]
                            V(lambda e: e.tensor_tensor(out=ta[0][:], in0=ps[:, b0, :], in1=kre, op=ALU.mult), [pb[b0], kldb[s_]], [tab[0]])
                            V(lambda e: e.tensor_tensor(out=ta[1][:], in0=ps[:, b1, :], in1=kim, op=ALU.mult), [pb[b1], kldb[s_]], [tab[1]])
                            V(lambda e: e.tensor_tensor(out=Yre[:, fch, :], in0=ta[0][:], in1=ta[1][:], op=ALU.add), [tab[0], tab[1]], [Yreb])
                            V(lambda e: e.tensor_tensor(out=ta[2][:], in0=ps[:, b0, :], in1=kim, op=ALU.mult), [pb[b0], kldb[s_]], [tab[2]])
                            V(lambda e: e.tensor_tensor(out=ta[3][:], in0=ps[:, b1, :], in1=kre, op=ALU.mult), [pb[b1], kldb[s_]], [tab[3]])
                            V(lambda e: e.tensor_tensor(out=Yim[:, fch, :], in0=ta[2][:], in1=ta[3][:], op=ALU.subtract), [tab[2], tab[3]], [Yimb])
                        S.barrier()
                    with ExitStack() as st:
                        cit = [sb(st, "cit%d" % i, [128, 4, 512], BF16) for i in range(4)]
                        sit = [sb(st, "sit%d" % i, [128, 4, 512], BF16) for i in range(4)]
                        uxt = sb(st, "uxt", [128, 4, 512])
                        yh = sb(st, "yh", [128, 4, 512])
                        sq = [sb(st, "hsq%d" % i, [128, 512], BF16) for i in range(2)]
                        rs = sb(st, "hrs", [128, 512])
                        yo = [sb(st, "hyo%d" % i, [128, 4, 512], BF16) for i in range(2)]
                        citb = [Buf() for _ in range(4)]
                        sitb = [Buf() for _ in range(4)]
                        uxb, rsb = Buf(), Buf()
                        yhb = [Buf() for _ in range(4)]
                        sqb = [Buf(), Buf()]
                        yob = [Buf(), Buf()]
                        k_ = 0
                        for tt in range(8):
                            tsl = slice(tt * 512, (tt + 1) * 512)
                            for fg8 in range(8):
                                s_ = k_ % 4
                                k_ += 1
                                fg, hh = fg8 // 2, fg8 % 2
                                S.dma(S.sp, cit[s_][:], cinv_d[tt, fg][:, hh * 2048:(hh + 1) * 2048].rearrange("p (k t) -> p k t", k=4), citb[s_], writes=[citb[s_]])
                                S.dma(S.sp, sit[s_][:], sinv_d[tt, fg][:, hh * 2048:(hh + 1) * 2048].rearrange("p (k t) -> p k t", k=4), sitb[s_], writes=[sitb[s_]])
                                for cc in range(4):
                                    mms = []
                                    for fc in range(4):
                                        f = fg8 * 4 + fc
                                        mms.append((ps[:, cc, :], Yre[:, f, cc * 128:(cc + 1) * 128], cit[s_][:, fc, :], fg8 == 0 and fc == 0, False))
                                        mms.append((ps[:, cc, :], Yim[:, f, cc * 128:(cc + 1) * 128], sit[s_][:, fc, :], False, fg8 == 7 and fc == 3))
                                    S.mm([pb[cc]], mms, [citb[s_], sitb[s_], Yreb, Yimb])
                            S.dma(S.act, uxt[:], ux0[:, tsl].rearrange("(c p) t -> p c t", p=128), uxb, reads=ux0_b, writes=[uxb])
                            nbk = 4 + tt % 2
                            for cc in range(4):
                                V(lambda e: e.tensor_tensor(out=yh[:, cc, :], in0=ps[:, cc, :], in1=uxt[:, cc, :], op=ALU.mult), [pb[cc], uxb], [yhb[cc]])
                                A(lambda e: e.activation(out=sq[cc % 2][:], in_=yh[:, cc, :], func=AF.Square), [yhb[cc]], [sqb[cc % 2]])
                                S.mm([pb[nbk]], [(ps[:, nbk, :], ones[:], sq[cc % 2][:], cc == 0, cc == 3)], [sqb[cc % 2], cb])
                            rsqrt(rs[:], ps[:, nbk, :], 1.0 / 512, [pb[nbk]], rsb)
                            o_ = tt % 2
                            for cc in range(4):
                                V(lambda e: e.scalar_tensor_tensor(out=yo[o_][:, cc, :], in0=yh[:, cc, :], scalar=sm[:, C_GHS + cc:C_GHS + cc + 1], in1=rs[:], op0=ALU.mult, op1=ALU.mult), [yhb[cc], rsb, smb[l]], [yob[o_]])
                            S.dma(S.act, ymix[0:512, tsl].rearrange("(c p) t -> p c t", p=128), yo[o_][:], yob[o_], reads=[yob[o_]], writes=ymix_b[0:4])
                        S.barrier()

        def sconv_units(l, st):
            sm = smt[l]
            H = L // 2
            ysc = [sb(st, "ysc%d" % i, [128, H]) for i in range(4)]
            pad = [sb(st, "scpad%d" % i, [128, H + 2]) for i in range(2)]
            ldH = [sb(st, "ldH%d" % i, [128, H + 2]) for i in range(2)]
            sq = [sb(st, "ssq%d" % i, [128, 512], BF16) for i in range(2)]
            rs = sb(st, "srs", [128, 512])
            yo = [sb(st, "syo%d" % i, [128, 4, 512], BF16) for i in range(2)]
            rsb = Buf()
            padb = [Buf(), Buf()]
            ldHb = [Buf(), Buf()]
            yscb = [Buf() for _ in range(4)]
            sqb = [Buf(), Buf()]
            yob = [Buf(), Buf()]

            def loads(hh, j):
                s_ = j % 2
                t0 = hh * H
                S.dma(S.pool, ysc[j][:], proj[(12 + j) * 128:(13 + j) * 128, t0:t0 + H], yscb[j], reads=[proj_b[12 + j]], writes=[yscb[j]])
                for (buf, bufb, ch) in ((pad[s_], padb[s_], 16 + j), (ldH[s_], ldHb[s_], 20 + j)):
                    if hh == 0:
                        V(lambda e: e.memset(buf[:, 0:1], 0.0), [], [bufb])
                        S.dma(S.pool, buf[:, 1:H + 2], proj[ch * 128:(ch + 1) * 128, 0:H + 1], bufb, reads=[proj_b[ch]], writes=[bufb])
                    else:
                        V(lambda e: e.memset(buf[:, H + 1:H + 2], 0.0), [], [bufb])
                        S.dma(S.pool, buf[:, 0:H + 1], proj[ch * 128:(ch + 1) * 128, H - 1:L], bufb, reads=[proj_b[ch]], writes=[bufb])

            for hh in range(2):
                loads(hh, 0)
                for j in range(4):
                    s_ = j % 2
                    if j + 1 < 4:
                        loads(hh, j + 1)
                    V(lambda e: e.tensor_tensor(out=pad[s_][:], in0=pad[s_][:], in1=ldH[s_][:], op=ALU.mult), [padb[s_], ldHb[s_]], [padb[s_]])
                    c0 = C_SCW + j * 3
                    conv3(pad[s_], ldH[s_], padb[s_], ldHb[s_], sm[:, c0:c0 + 1], sm[:, c0 + 1:c0 + 2], sm[:, c0 + 2:c0 + 3], None, n=H)
                    V(lambda e: e.tensor_tensor(out=ysc[j][:], in0=ysc[j][:], in1=ldH[s_][:, 0:H], op=ALU.mult), [yscb[j], ldHb[s_]], [yscb[j]])
                    yield
                for t4 in range(4):
                    tt = hh * 4 + t4
                    lsl = slice(t4 * 512, (t4 + 1) * 512)
                    tsl = slice(tt * 512, (tt + 1) * 512)
                    nbk = t4 % 2
                    for cc in range(4):
                        A(lambda e: e.activation(out=sq[cc % 2][:], in_=ysc[cc][:, lsl], func=AF.Square), [yscb[cc]], [sqb[cc % 2]])
                        S.mm([pb[nbk]], [(ps[:, nbk, :], ones[:], sq[cc % 2][:], cc == 0, cc == 3)], [sqb[cc % 2], cb])
                    rsqrt(rs[:], ps[:, nbk, :], 1.0 / 512, [pb[nbk]], rsb)
                    o_ = tt % 2
                    for cc in range(4):
                        V(lambda e: e.scalar_tensor_tensor(out=yo[o_][:, cc, :], in0=ysc[cc][:, lsl], scalar=sm[:, C_GHS + 4 + cc:C_GHS + 5 + cc], in1=rs[:], op0=ALU.mult, op1=ALU.mult), [yscb[cc], rsb, smb[l]], [yob[o_]])
                    S.dma(S.pool, ymix[512:1024, tsl].rearrange("(c p) t -> p c t", p=128), yo[o_][:], yob[o_], reads=[yob[o_]], writes=ymix_b[4:8])
                    yield

        def mx_attn(l):
            sm = smt[l]
            w = W[l]
            VA = 68
            with ExitStack() as st:
                qn = sb(st, "qn", [128, 8, L], BF16)
                kd = sb(st, "kd", [128, 4, L], BF16)
                vaug = sb(st, "vaug", [128, 32, 4, VA], BF16)
                ld = [sb(st, "ald%d" % i, [128, 1024]) for i in range(2)]
                sqa = [sb(st, "asq%d" % i, [128, 1024], BF16) for i in range(2)]
                rs = sb(st, "ars", [128, 1024])
                gat = sb(st, "gatt", [128, 1024])
                dm = sb(st, "dmt", [128, 384])
                tmp = [sb(st, "atmp%d" % i, [128, 384]) for i in range(8)]
                pt = [sb(st, "apt%d" % i, [128, 384], BF16) for i in range(8)]
                yat = [sb(st, "yat%d" % i, [128, 1024]) for i in range(2)]
                ybf = [sb(st, "ybf%d" % i, [128, 1024], BF16) for i in range(2)]
                junk = sb(st, "ajunk", [128, 1024], BF16)
                yts = [sb(st, "yts%d" % i, [128, 8, 512], BF16) for i in range(2)]
                esink = sb(st, "esink", [128, 16])
                den = sb(st, "aden", [128, 8])
                rec = sb(st, "arec", [128, 8])
                ssq = sb(st, "assq", [128, 2])
                qnb, kdb, vaugb, rsb, gatb, dmb, junkb, esb = [Buf() for _ in range(8)]
                denb = [Buf(), Buf()]
                recb = [Buf(), Buf()]
                ssqb = [Buf(), Buf()]
                ldb = [Buf(), Buf()]
                sqab = [Buf(), Buf()]
                tmpb = [Buf() for _ in range(8)]
                ptb = [Buf() for _ in range(8)]
                yatb = [[Buf() for _ in range(4)] for _ in range(2)]
                ybfb = [Buf(), Buf()]
                ytsb = [Buf(), Buf()]
                S.dma(S.sp, gat[:], w["gat"], gatb, writes=[gatb])
                S.dma(S.sp, dm[:], dmat_d, dmb, writes=[dmb])
                A(lambda e: e.activation(out=esink[:], in_=sm[:, C_SINK:C_SINK + 16], func=AF.Exp), [smb[l]], [esb])
                V(lambda e: e.memset(vaug[:, :, :, 64:65], 1.0), [], [vaugb])
                k_ = 0
                qcb = [Buf() for _ in range(12)]
                for ch in range(12):
                    dst = qn if ch < 8 else kd
                    dstb = qnb if ch < 8 else kdb
                    dch = ch if ch < 8 else ch - 8
                    S.dma(S.sp, dst[:, dch, :], qkn[ch * 128:(ch + 1) * 128, :], qcb[ch], reads=[qkn_b[ch]], writes=[qcb[ch]])
                for pc in range(8):
                    s_ = k_ % 2
                    k_ += 1
                    S.dma(S.sp, ld[s_][:].rearrange("p (b c) -> p b c", b=4), vtm[pc * 512:(pc + 1) * 512, :].rearrange("(b p) c -> p b c", p=128), ldb[s_], reads=[vtm_b], writes=[ldb[s_]])
                    V(lambda e: e.tensor_copy(out=vaug[:, pc * 4:(pc + 1) * 4, :, 0:64], in_=ld[s_][:].rearrange("p (b g d) -> p b g d", b=4, g=4)), [ldb[s_]], [vaugb])
                def jl_of(i):
                    return [jj for jj in range(3) if 0 <= i - 1 + jj < 32]

                def s1(i, g):
                    jl = jl_of(i)
                    for r in range(4):
                        h = 4 * g + r
                        po = (r % 2) * 64
                        qc = h // 2
                        S.mm([pb[r]], [(ps[:, r, jj * 128:(jj + 1) * 128], kd[po:po + 64, g, (i - 1 + jj) * 128:(i + jj) * 128], qn[po:po + 64, qc, i * 128:(i + 1) * 128], True, True) for jj in jl], [qcb[8 + g], qcb[qc]])

                def s2(i, g):
                    jl = jl_of(i)
                    lo, hi = jl[0] * 128, (jl[-1] + 1) * 128
                    for r in range(4):
                        h = 4 * g + r
                        b_ = (g % 2) * 4 + r
                        slope8 = 8.0 * (2.0 ** (-(h + 1) / 2.0))
                        V(lambda e: e.scalar_tensor_tensor(out=tmp[b_][:, lo:hi], in0=dm[:, lo:hi], scalar=slope8, in1=ps[:, r, lo:hi], op0=ALU.mult, op1=ALU.add), [dmb, pb[r]], [tmpb[b_]])
                        A(lambda e: e.activation(out=pt[b_][:, lo:hi], in_=tmp[b_][:, lo:hi], func=AF.Exp, scale=0.125), [tmpb[b_]], [ptb[b_]])

                def s3(i, g):
                    jl = jl_of(i)
                    ob = 4 + g % 2
                    y_ = i % 2
                    for r in range(4):
                        b_ = (g % 2) * 4 + r
                        S.mm([pb[ob]], [(ps[:, ob, r * 128:r * 128 + 65], pt[b_][:, jj * 128:(jj + 1) * 128], vaug[:, i - 1 + jj, g, 0:65], jj == jl[0], jj == jl[-1]) for jj in jl], [ptb[b_], vaugb])
                    d_ = g % 2
                    V(lambda e: e.tensor_tensor(out=den[:, d_ * 4:d_ * 4 + 4].rearrange("p (a b) -> p a b", b=1), in0=ps[:, ob, :].rearrange("p (a b) -> p a b", b=128)[:, :, 64:65], in1=esink[:, 4 * g:4 * g + 4].rearrange("p (a b) -> p a b", b=1), op=ALU.add), [pb[ob], esb], [denb[d_]])
                    V(lambda e: e.reciprocal(out=rec[:, d_ * 4:d_ * 4 + 4], in_=den[:, d_ * 4:d_ * 4 + 4]), [denb[d_]], [recb[d_]])
                    for r in range(4):
                        h = 4 * g + r
                        if g % 2 == 0:
                            A(lambda e: e.activation(out=yat[y_][:, h * 64:(h + 1) * 64], in_=ps[:, ob, r * 128:r * 128 + 64], func=AF.Copy, scale=rec[:, d_ * 4 + r:d_ * 4 + r + 1]), [pb[ob], recb[d_]], [yatb[y_][g]])
                        else:
                            V(lambda e: e.tensor_scalar(out=yat[y_][:, h * 64:(h + 1) * 64], in0=ps[:, ob, r * 128:r * 128 + 64], scalar1=rec[:, d_ * 4 + r:d_ * 4 + r + 1], scalar2=None, op0=ALU.mult), [pb[ob], recb[d_]], [yatb[y_][g]])

                def norm_ew(i):
                    y_ = i % 2
                    A(lambda e: e.activation(out=junk[:], in_=yat[y_][:], func=AF.Square, accum_out=ssq[:, y_:y_ + 1]), yatb[y_], [junkb, ssqb[y_]])
                    rsqrt(ssq[:, y_:y_ + 1], ssq[:, y_:y_ + 1], 1.0 / 1024, [ssqb[y_]], ssqb[y_])
                    V(lambda e: e.scalar_tensor_tensor(out=ybf[y_][:], in0=yat[y_][:], scalar=ssq[:, y_:y_ + 1], in1=gat[:], op0=ALU.mult, op1=ALU.mult), yatb[y_] + [ssqb[y_], gatb], [ybfb[y_]])

                def norm_tr(i):
                    y_ = i % 2
                    tb = 6 + i % 2
                    psT = ps[:, tb, :].bitcast(BF16)
                    S.tr([pb[tb]], [(psT[:, c * 128:(c + 1) * 128], ybf[y_][:, c * 128:(c + 1) * 128], identb[:]) for c in range(8)], [ybfb[y_], idb[0]])
                    t_ = (i // 4) % 2
                    A(lambda e: e.copy(out=yts[t_][:, :, (i % 4) * 128:(i % 4 + 1) * 128], in_=psT.rearrange("p (c t) -> p c t", c=8)), [pb[tb]], [ytsb[t_]])
                    if i % 4 == 3:
                        tsl = slice((i // 4) * 512, (i // 4 + 1) * 512)
                        S.dma(S.sp, ymix[1024:2048, tsl].rearrange("(c p) t -> p c t", p=128), yts[t_][:], ytsb[t_], reads=[ytsb[t_]], writes=ymix_b[8:16])

                s1(0, 0)
                for i in range(32):
                    for g in range(4):
                        s2(i, g)
                        if g < 3:
                            s1(i, g + 1)
                        elif i < 31:
                            s1(i + 1, 0)
                        s3(i, g)
                        if g == 1 and i > 0:
                            norm_tr(i - 1)
                    norm_ew(i)
                norm_tr(31)
                S.barrier()

        def mx_pass(l):
            mx_filter(l)
            mx_hyena(l)
            mx_attn(l)

        S.barrier()
        steps = [
            lambda: tl_pass(None, 0, xT, None, xres, xres_b),
            lambda: mx_pass(0),
            lambda: tl_pass(0, 1, xres, xres_b, xres, xres_b),
            lambda: mx_pass(1),
            lambda: tl_pass(1, None, xres, xres_b, yout, None),
        ]
        for i, fn in enumerate(steps):
            if stop_after is not None and i > stop_after:
                break
            fn()
        S.barrier()
    return nc


_CONST = {}


def _constants():
    if _CONST:
        return _CONST
    bf = ml_dtypes.bfloat16
    n = np.arange(L, dtype=np.float32)
    t = n / np.float32(L - 1)
    bands = np.linspace(1e-4, 15, 16, dtype=np.float32)
    ang = (np.float32(2.0 * math.pi / L) * n[:, None] * bands[None, :]).astype(np.float32)
    z = np.concatenate([t[:, None], np.cos(ang), np.sin(ang)], axis=-1).astype(np.float32)
    _CONST["zT"] = np.ascontiguousarray(z.T)
    min_decay = math.log(1e-2) / 1.5
    max_decay = math.log(1e-2) / 0.3
    deltas = np.abs(np.linspace(min_decay, max_decay, 512, dtype=np.float32))
    _CONST["winw"] = (np.exp(-t[:, None] * deltas[None, :]) + np.float32(0.05)).astype(np.float32)
    tt = np.arange(L, dtype=np.int64)
    ff = np.arange(L, dtype=np.int64)
    ph = ((2 * ff[None, :] + 1) * tt[:, None]) % 16384
    angm = ph.astype(np.float64) * (2.0 * math.pi / 16384.0)
    C = np.cos(angm).astype(np.float32)
    Sm = np.sin(angm).astype(np.float32)
    del angm, ph

    def fw(M):
        return np.ascontiguousarray(M.reshape(32, 128, 32, 128).transpose(2, 1, 0, 3).reshape(32, 128, 4096)).astype(bf)

    def inv(M):
        return np.ascontiguousarray(M.reshape(8, 512, 4, 8, 128).transpose(0, 2, 4, 3, 1).reshape(8, 4, 128, 4096)).astype(bf)

    _CONST["cfw"] = fw(C)
    _CONST["sfw"] = fw(Sm)
    _CONST["cinv"] = inv(C)
    _CONST["sinv"] = inv(-Sm)
    s = np.arange(128)[:, None]
    col = np.arange(384)[None, :]
    jj = col // 128
    q = col % 128
    dist = np.abs(128 * (jj - 1) + s - q)
    _CONST["dmat"] = np.where(dist <= 128, -dist.astype(np.float32), np.float32(-1.0e6)).astype(np.float32)
    _CONST["identb"] = np.eye(128, dtype=np.float32).astype(bf)
    _CONST["identf"] = np.eye(128, dtype=np.float32)
    return _CONST


def _tile_w(w, cols=None):
    K, N = w.shape
    return np.ascontiguousarray(w.reshape(K // 128, 128, N // 128, 128).transpose(2, 1, 0, 3).reshape(N // 128, 128, (K // 128) * 128))


def _col(v):
    return np.ascontiguousarray(np.asarray(v, np.float32).reshape(-1, 128).T)


def _shared_inputs(inp):
    m = dict(_constants())
    for l in range(DEPTH):
        for f, pre in ((1, "ffn1"), (2, "ffn2")):
            m["wg%d_%d" % (f, l)] = _tile_w(inp[pre + "_w_gate"][l])
            m["wu%d_%d" % (f, l)] = _tile_w(inp[pre + "_w_up"][l])
            m["wd%d_%d" % (f, l)] = np.ascontiguousarray(inp[pre + "_w_down"][l])
        win = inp["w_in"][l]
        kcols = win[:, 4096:4352]
        kdup = np.concatenate([np.concatenate([kcols[:, g * 64:(g + 1) * 64]] * 2, axis=1) for g in range(4)], axis=1)
        wfm = np.concatenate([win[:, :4096], kdup], axis=1)
        m["win_%d" % l] = _tile_w(wfm)
        wv = win[:, 4352:4608]
        m["winv_%d" % l] = np.ascontiguousarray(wv.reshape(16, 128, 256).transpose(1, 0, 2).reshape(128, 16 * 256))
        m["wout_%d" % l] = _tile_w(inp["w_out"][l])
        sm = np.zeros((128, NSM), np.float32)
        sm[:, C_GF1:C_GF1 + 16] = _col(inp["norm_ffn1"][l])
        sm[:, C_GMIX:C_GMIX + 16] = _col(inp["norm_mix"][l])
        sm[:, C_GF2:C_GF2 + 16] = _col(inp["norm_ffn2"][l])
        hsw = inp["hyena_short_w"][l]
        for tap in range(3):
            sm[:, C_HSW + tap:C_HSW + 36:3] = _col(hsw[tap])
        sm[:, C_HSB:C_HSB + 12] = _col(inp["hyena_short_b"][l])
        scw = inp["sconv_w"][l]
        for tap in range(3):
            sm[:, C_SCW + tap:C_SCW + 12:3] = _col(scw[tap])
        sm[:, C_GHS:C_GHS + 8] = _col(inp["mix_out_g"][l][:1024])
        sm[:, C_GQ] = np.concatenate([inp["q_norm_g"][l]] * 2)
        sm[:, C_GK] = np.concatenate([inp["k_norm_g"][l]] * 2)
        sm[:64, C_FREQ] = inp["filt_freq"][l]
        sm[:64, C_B1] = inp["filt_b1"][l]
        sm[:64, C_B2] = inp["filt_b2"][l]
        sm[:64, C_B3] = inp["filt_b3"][l]
        sm[:, C_SINK:C_SINK + 16] = np.broadcast_to(inp["attn_sink"][l][None, :], (128, 16))
        m["sm_%d" % l] = sm
        m["gat_%d" % l] = np.ascontiguousarray(np.broadcast_to(inp["mix_out_g"][l][None, 1024:], (128, 1024)))
        m["hbias_%d" % l] = np.ascontiguousarray(inp["hyena_bias"][l][None, :])
        m["fw1_%d" % l] = np.ascontiguousarray(inp["filt_w1"][l])
        m["fw2_%d" % l] = np.ascontiguousarray(inp["filt_w2"][l])
        m["fw3_%d" % l] = np.ascontiguousarray(inp["filt_w3"][l])
        m["fwo_%d" % l] = np.ascontiguousarray(inp["filt_w_out"][l])
    return m


def kernel(**inputs):
    inp = {k: np.asarray(v, dtype=np.float32) for k, v in inputs.items()}
    shared = _shared_inputs(inp)
    x = inp["x"]
    in_maps = []
    for b in range(NCORES):
        m = dict(shared)
        m["xT"] = np.ascontiguousarray(x[b].T)
        in_maps.append(m)
    nc = build_nc()
    res = run_bass_kernel_spmd(nc, in_maps, core_ids=list(range(NCORES)))
    out = np.stack([np.ascontiguousarray(res.results[b]["yout"].T) for b in range(NCORES)], axis=0)
    return out.astype(np.float32)
```
